# Optimizing a Trainium2 kernel written in Bass

```python
import jax, jax.numpy as jnp
from jax import lax
import numpy as np

D_MODEL = 2048
BATCH = 2
SEQ = 8192
DEPTH = 2

MIX_WIDTH = D_MODEL
DN_HEADS = 8
DN_HEAD_DIM = (MIX_WIDTH // 2) // DN_HEADS
DN_WIDTH = DN_HEADS * DN_HEAD_DIM
CONV_K = 4
CHUNK = 64
AT_HEAD_DIM = 64
AT_Q_HEADS = (MIX_WIDTH - DN_WIDTH) // AT_HEAD_DIM
AT_KV_HEADS = 2
AT_WIDTH = AT_Q_HEADS * AT_HEAD_DIM
AT_KV_WIDTH = AT_KV_HEADS * AT_HEAD_DIM
WINDOW = 128
ROPE_THETA = 10000.0
COL_SIZES = (3 * DN_WIDTH, DN_WIDTH, DN_HEADS, DN_HEADS, AT_WIDTH, AT_KV_WIDTH, AT_KV_WIDTH)
IN_COLS = 3 * DN_WIDTH + DN_WIDTH + 2 * DN_HEADS + AT_WIDTH + 2 * AT_KV_WIDTH
FFN_DIM = ((8 * D_MODEL + 3 * 256 - 1) // (3 * 256)) * 256
N_MOD = 6
EPS = 1e-6

kernel_name = "hybrid_deltanet_swa_sink_adaln_block"


def rms_norm(x, gain):
    xf = x.astype(jnp.float32)
    y = xf * lax.rsqrt(jnp.mean(xf * xf, axis=-1, keepdims=True) + EPS)
    return (y * gain.astype(jnp.float32)).astype(x.dtype)


def l2_norm(x):
    return x * lax.rsqrt(jnp.sum(x * x, axis=-1, keepdims=True) + EPS)


def rope(x, pos):
    d = x.shape[-1]
    half = d // 2
    inv_freq = ROPE_THETA ** (-jnp.arange(half, dtype=jnp.float32) * 2.0 / d)
    ang = pos.astype(jnp.float32)[:, None] * inv_freq[None, :]
    cos = jnp.cos(ang)[None, :, None, :]
    sin = jnp.sin(ang)[None, :, None, :]
    xf = x.astype(jnp.float32)
    x1, x2 = xf[..., :half], xf[..., half:]
    return jnp.concatenate([x1 * cos - x2 * sin, x2 * cos + x1 * sin], axis=-1).astype(x.dtype)


def to_chunks(t):
    b, tl, h = t.shape[:3]
    t = t.reshape(b, tl // CHUNK, CHUNK, h, *t.shape[3:])
    return jnp.moveaxis(t, 3, 1)


def chunk_gated_delta_rule(q, k, v, g, beta):
    b, tl, h, dv = v.shape
    qc, kc, vc = to_chunks(q), to_chunks(k), to_chunks(v)
    gc = jnp.cumsum(to_chunks(g), axis=-1)
    bc = to_chunks(beta)[..., None]
    causal = jnp.tril(jnp.ones((CHUNK, CHUNK), dtype=bool))
    strict = jnp.tril(jnp.ones((CHUNK, CHUNK), dtype=bool), -1)
    decay = jnp.exp(jnp.where(causal, gc[..., :, None] - gc[..., None, :], -jnp.inf))
    kb = kc * bc
    vb = vc * bc
    eye = jnp.eye(CHUNK, dtype=jnp.float32)
    lower = jnp.where(strict, jnp.einsum('bhncd,bhnsd->bhncs', kb, kc) * decay, 0.0) + eye
    t_inv = lax.linalg.triangular_solve(lower, jnp.broadcast_to(eye, lower.shape),
                                        left_side=True, lower=True, unit_diagonal=True)
    w = jnp.einsum('bhncs,bhnsd->bhncd', t_inv, kb * jnp.exp(gc)[..., None])
    u = jnp.einsum('bhncs,bhnsd->bhncd', t_inv, vb)
    intra = jnp.where(causal, jnp.einsum('bhncd,bhnsd->bhncs', qc, kc) * decay, 0.0)
    qg = qc * jnp.exp(gc)[..., None]
    kd = kc * jnp.exp(gc[..., -1:] - gc)[..., None]
    glast = jnp.exp(gc[..., -1])

    def step(state, inp):
        w_i, u_i, qg_i, intra_i, kd_i, gl_i = inp
        v_new = u_i - jnp.einsum('bhck,bhkv->bhcv', w_i, state)
        o_i = jnp.einsum('bhck,bhkv->bhcv', qg_i, state) + jnp.einsum('bhcs,bhsv->bhcv', intra_i, v_new)
        state = state * gl_i[..., None, None] + jnp.einsum('bhck,bhcv->bhkv', kd_i, v_new)
        return state, o_i

    seq_first = lambda t: jnp.moveaxis(t, 2, 0)
    state0 = jnp.zeros((b, h, q.shape[-1], dv), jnp.float32)
    _, o = lax.scan(step, state0, (seq_first(w), seq_first(u), seq_first(qg), seq_first(intra),
                                   seq_first(kd), jnp.moveaxis(glast, 2, 0)))
    o = jnp.moveaxis(jnp.moveaxis(o, 0, 2), 1, 3)
    return o.reshape(b, tl, h, dv)


def gated_deltanet(qkv, z, b_raw, a_raw, conv_w, a_log, dt_bias, norm_w):
    bsz, tl, _ = qkv.shape
    conv = lax.conv_general_dilated(qkv, conv_w[:, None, :].astype(qkv.dtype), window_strides=(1,),
                                    padding=[(CONV_K - 1, 0)], dimension_numbers=('NWC', 'WIO', 'NWC'),
                                    feature_group_count=3 * DN_WIDTH)
    conv = jax.nn.silu(conv.astype(jnp.float32))
    q, k, v = jnp.split(conv, 3, axis=-1)
    shp = (bsz, tl, DN_HEADS, DN_HEAD_DIM)
    q = l2_norm(q.reshape(shp)) * (DN_HEAD_DIM ** -0.5)
    k = l2_norm(k.reshape(shp))
    v = v.reshape(shp)
    beta = jax.nn.sigmoid(b_raw.astype(jnp.float32))
    g = -jnp.exp(a_log.astype(jnp.float32)) * jax.nn.softplus(a_raw.astype(jnp.float32) + dt_bias.astype(jnp.float32))
    o = chunk_gated_delta_rule(q, k, v, g, beta)
    o = o * lax.rsqrt(jnp.mean(o * o, axis=-1, keepdims=True) + EPS) * norm_w.astype(jnp.float32)
    o = o * jax.nn.silu(z.astype(jnp.float32).reshape(shp))
    return o.reshape(bsz, tl, DN_WIDTH).astype(qkv.dtype)


def sliding_window_attention_sinks(q, k, v, sinks):
    bsz, tl, _, dh = q.shape
    nb = tl // WINDOW
    grp = AT_Q_HEADS // AT_KV_HEADS
    qb = q.reshape(bsz, nb, WINDOW, AT_KV_HEADS, grp, dh)
    def band(t):
        tb = t.reshape(bsz, nb, WINDOW, AT_KV_HEADS, dh)
        prev = jnp.pad(tb, ((0, 0), (1, 0), (0, 0), (0, 0), (0, 0)))[:, :-1]
        return jnp.concatenate([prev, tb], axis=2)
    kw, vw = band(k), band(v)
    s = jnp.einsum('bnqhgd,bnkhd->bnhgqk', qb, kw).astype(jnp.float32) * (dh ** -0.5)
    r = jnp.arange(WINDOW)[:, None]
    j = jnp.arange(2 * WINDOW)[None, :]
    in_band = (j > r) & (j <= r + WINDOW)
    valid = (jnp.arange(nb)[:, None, None] > 0) | (j >= WINDOW)[None]
    mask = (in_band[None] & valid)[None, :, None, None]
    s = jnp.where(mask, s, -jnp.inf)
    sink = sinks.astype(jnp.float32).reshape(AT_KV_HEADS, grp)[None, None, :, :, None, None]
    m = jnp.maximum(jnp.max(s, axis=-1, keepdims=True), sink)
    p = jnp.exp(s - m)
    p = p / (jnp.sum(p, axis=-1, keepdims=True) + jnp.exp(sink - m))
    o = jnp.einsum('bnhgqk,bnkhd->bnqhgd', p.astype(v.dtype), vw)
    return o.reshape(bsz, tl, AT_WIDTH)


def setup_inputs(seed: int = 0) -> dict:
    key = jax.random.key(seed)
    ks = jax.random.split(key, 20)
    f32 = jnp.float32
    nrm = lambda k, shp, s: jax.random.normal(k, shp, f32) * s
    return {
        "x": nrm(ks[0], (BATCH, SEQ, D_MODEL), 1.0),
        "c": nrm(ks[1], (BATCH, D_MODEL), 1.0),
        "ln_mix": 1.0 + nrm(ks[2], (DEPTH, D_MODEL), 0.02),
        "ln_ffn": 1.0 + nrm(ks[3], (DEPTH, D_MODEL), 0.02),
        "w_ada": nrm(ks[4], (DEPTH, D_MODEL, N_MOD * D_MODEL), D_MODEL ** -0.5),
        "b_ada": nrm(ks[5], (DEPTH, N_MOD * D_MODEL), 0.02),
        "w_in": nrm(ks[6], (DEPTH, D_MODEL, IN_COLS), D_MODEL ** -0.5),
        "dn_conv_w": nrm(ks[7], (DEPTH, CONV_K, 3 * DN_WIDTH), CONV_K ** -0.5),
        "dn_a_log": jnp.log(jax.random.uniform(ks[8], (DEPTH, DN_HEADS), f32, 1.0, 16.0)),
        "dn_dt_bias": jnp.log(jnp.expm1(jax.random.uniform(ks[9], (DEPTH, DN_HEADS), f32, 0.001, 0.1))),
        "dn_norm_w": 1.0 + nrm(ks[10], (DEPTH, DN_HEAD_DIM), 0.02),
        "attn_sinks": nrm(ks[11], (DEPTH, AT_Q_HEADS), 0.5),
        "w_out": nrm(ks[12], (DEPTH, MIX_WIDTH, D_MODEL), MIX_WIDTH ** -0.5),
        "w_gate_up": nrm(ks[13], (DEPTH, D_MODEL, 2 * FFN_DIM), D_MODEL ** -0.5),
        "w_down": nrm(ks[14], (DEPTH, FFN_DIM, D_MODEL), FFN_DIM ** -0.5),
        "ln_final": 1.0 + nrm(ks[15], (D_MODEL,), 0.02),
    }


def reference(x, c, ln_mix, ln_ffn, w_ada, b_ada, w_in, dn_conv_w, dn_a_log, dn_dt_bias,
              dn_norm_w, attn_sinks, w_out, w_gate_up, w_down, ln_final):
    bsz, tl, _ = x.shape
    pos = jnp.arange(tl, dtype=jnp.int32)
    split_at = np.cumsum(COL_SIZES)[:-1].tolist()
    c_act = jax.nn.silu(c)
    for l in range(DEPTH):
        mod = c_act @ w_ada[l] + b_ada[l]
        sh_m, sc_m, gt_m, sh_f, sc_f, gt_f = [t[:, None, :] for t in jnp.split(mod, N_MOD, axis=-1)]
        h = rms_norm(x, ln_mix[l]) * (1.0 + sc_m) + sh_m
        proj = h @ w_in[l]
        dn_qkv, dn_z, dn_b, dn_a, at_q, at_k, at_v = jnp.split(proj, split_at, axis=-1)
        dn_out = gated_deltanet(dn_qkv, dn_z, dn_b, dn_a, dn_conv_w[l], dn_a_log[l],
                                dn_dt_bias[l], dn_norm_w[l])
        q = rope(at_q.reshape(bsz, tl, AT_Q_HEADS, AT_HEAD_DIM), pos)
        k = rope(at_k.reshape(bsz, tl, AT_KV_HEADS, AT_HEAD_DIM), pos)
        v = at_v.reshape(bsz, tl, AT_KV_HEADS, AT_HEAD_DIM)
        at_out = sliding_window_attention_sinks(q, k, v, attn_sinks[l])
        mix = jnp.concatenate([dn_out, at_out], axis=-1) @ w_out[l]
        x = x + gt_m * mix
        h = rms_norm(x, ln_ffn[l]) * (1.0 + sc_f) + sh_f
        gate, up = jnp.split(h @ w_gate_up[l], 2, axis=-1)
        x = x + gt_f * ((jax.nn.silu(gate) * up) @ w_down[l])
    return rms_norm(x, ln_final)
```

```python
import numpy as np
import ml_dtypes
from contextlib import ExitStack
import concourse.bass as bass
import concourse.mybir as mybir
from concourse.bass_utils import run_bass_kernel_spmd


ENGS = ("pe", "act", "dve", "pool", "sp")


class _Op:
    __slots__ = ("eng", "fn", "deps", "chan", "marked", "val")

    def __init__(self, eng, fn, deps, chan=None):
        self.eng, self.fn, self.deps, self.chan = eng, fn, deps, chan
        self.marked = False
        self.val = None


class Sched:
    def __init__(self, nc, stack):
        self.nc, self.stack = nc, stack
        self.ops = {e: [] for e in ENGS}
        self.psem = {e: stack.enter_context(nc.semaphore("prog_" + e)) for e in ENGS[:4]}
        self.csem, self.ccnt = {}, {}
        self.last_w, self.readers = {}, {}
        self.out_tokens = []
        self.rot = {}

    def _deps(self, reads, writes):
        d = []
        for r in reads:
            t = self.last_w.get(r)
            if t is not None:
                d.append(t)
        for w in writes:
            t = self.last_w.get(w)
            if t is not None:
                d.append(t)
            d.extend(self.readers.get(w, ()))
        return [("c", t[1], self.ccnt[t[1]]) if t[0] == "c" else t for t in d]

    def chan_tokens(self, base):
        return [("c", ch, n) for ch, n in self.ccnt.items() if ch == base or ch.startswith(base + "#")]

    def barrier(self):
        toks = []
        for e in ENGS[:4]:
            idx = [i for i, o in enumerate(self.ops[e]) if o.fn is not None and o.chan is None]
            if idx:
                toks.append(("e", e, idx[-1]))
        toks += [("c", ch, n) for ch, n in self.ccnt.items() if n]
        for e in ENGS:
            self.ops[e].append(_Op(e, None, list(toks)))

    def _record(self, tok, reads, writes):
        for r in reads:
            self.readers.setdefault(r, []).append(tok)
        for w in writes:
            self.last_w[w] = tok
            self.readers[w] = []

    PSUM_PREFIX = ("bk", "ptb", "pmm", "pss", "pmod")

    def op(self, eng, fn, reads=(), writes=()):
        ex = [r for r in reads if isinstance(r, str) and r.startswith(self.PSUM_PREFIX)]
        if ex:
            reads = [r for r in reads if r not in ex]
            writes = list(writes) + ex
        deps = self._deps(reads, writes)
        if eng == "pe":
            deps = [t for t in deps if not (t[0] == "e" and t[1] == "pe")]
        self.ops[eng].append(_Op(eng, fn, deps))
        tok = ("e", eng, len(self.ops[eng]) - 1)
        self._record(tok, reads, writes)
        return tok

    def dma(self, queue, out, in_, chan, reads=(), writes=(), **kw):
        if isinstance(chan, tuple):
            base, n = chan
            i = self.rot.get(base, 0)
            self.rot[base] = i + 1
            chan = "%s#%d" % (base, i % n)
        if chan not in self.csem:
            self.csem[chan] = self.stack.enter_context(self.nc.semaphore("ch_%d" % len(self.csem)))
            self.ccnt[chan] = 0
        deps = self._deps(reads, writes)
        if self.ccnt[chan]:
            deps.append(("c", chan, self.ccnt[chan]))
        self.ccnt[chan] += 16
        self.ops[queue].append(_Op(queue, lambda e: e.dma_start(out=out, in_=in_, **kw), deps, chan=chan))
        tok = ("c", chan, self.ccnt[chan])
        self._record(tok, reads, writes)
        return tok

    def emit(self, block, final_tokens=()):
        for e in ENGS:
            for o in self.ops[e]:
                for t in o.deps:
                    if t[0] == "e":
                        self.ops[t[1]][t[2]].marked = True
        for t in final_tokens:
            if t[0] == "e":
                self.ops[t[1]][t[2]].marked = True
        for e in ENGS:
            c = 0
            for o in self.ops[e]:
                if o.marked:
                    c += 1
                    o.val = c

        def resolve(t):
            if t[0] == "e":
                return self.psem[t[1]], self.ops[t[1]][t[2]].val, ("e", t[1])
            return self.csem[t[1]], t[2], ("c", t[1])

        def run(e, engobj, extra_final):
            waited = {}
            for o in self.ops[e]:
                need = {}
                for t in o.deps:
                    sem, v, k = resolve(t)
                    if waited.get(k, 0) >= v:
                        continue
                    if k not in need or need[k][1] < v:
                        need[k] = (sem, v)
                for k, (sem, v) in need.items():
                    engobj.wait_ge(sem, v)
                    waited[k] = v
                if o.fn is None:
                    continue
                ins = o.fn(engobj)
                if o.chan is not None:
                    ins.then_inc(self.csem[o.chan], 16)
                elif o.marked:
                    ins.then_inc(self.psem[e], 1)
            if extra_final:
                for t in final_tokens:
                    sem, v, k = resolve(t)
                    engobj.wait_ge(sem, v)

        @block.tensor
        def _(eng):
            run("pe", eng, False)

        @block.scalar
        def _(eng):
            run("act", eng, False)

        @block.vector
        def _(eng):
            run("dve", eng, False)

        @block.gpsimd
        def _(eng):
            run("pool", eng, False)

        @block.sync
        def _(eng):
            run("sp", eng, True)


F32 = mybir.dt.float32
BF16 = mybir.dt.bfloat16
AF = mybir.ActivationFunctionType
ALU = mybir.AluOpType
D = 2048
DC = 16
FFN = 5632
FC = 44
EPS = 1e-6


class Ctx:
    def __init__(self, nc, st, S):
        self.nc, self.st, self.S = nc, st, S
        self.n = 0

    def sb(self, shape, dt, name=None):
        self.n += 1
        return self.st.enter_context(self.nc.sbuf_tensor("%s_%d" % (name or "t", self.n), shape, dt))

    def ps(self, shape, dt=F32, name=None):
        self.n += 1
        return self.st.enter_context(self.nc.psum_tensor("%s_%d" % (name or "p", self.n), shape, dt))


def emit_mod(cx, c_col, wada_t, bada_col, mod_out, nlayers):
    S, nc = cx.S, cx.nc
    NJ = nlayers * 96
    cc = cx.sb([128, DC], F32, "cc")
    cact = cx.sb([128, DC, 2], F32, "cact")
    bada = cx.sb([128, NJ], F32, "bada")
    slabs = [cx.sb([128, 2048], F32, "wa%d" % i) for i in range(3)]
    pm = cx.ps([128, 512], F32, "pmod")
    S.dma("sp", cc[:], c_col, "misc", writes=["cc"])
    S.dma("sp", bada[:], bada_col, "misc", writes=["bada"])
    for i in range(2):
        S.op("act", lambda e, i=i: e.activation(out=cact[:, :, i], in_=cc[:], func=AF.Silu), reads=["cc"], writes=["cact"])
    for j in range(NJ):
        s = j % 3
        S.dma("sp", slabs[s][:], wada_t[j], "wa%d" % s, writes=["wa%d" % s])
        col = (j % 96) * 2
        for kc in range(DC):
            S.op("pe", lambda e, s=s, kc=kc, col=col: e.matmul(pm[:, col:col + 2], lhsT=slabs[s][:, kc * 128:(kc + 1) * 128],
                                                                 rhs=cact[:, kc, :], start=(kc == 0), stop=(kc == DC - 1)),
                 reads=["wa%d" % s, "cact"], writes=["pmod"])
        if j % 96 == 95:
            l = j // 96
            pv = pm[:, 0:192].rearrange("p (j t) -> p j t", t=2)[:, :, 0]
            S.op("dve", lambda e, l=l, pv=pv: e.tensor_tensor(out=mod_out[:, l * 96:(l + 1) * 96], in0=pv, in1=bada[:, l * 96:(l + 1) * 96], op=ALU.add),
                 reads=["pmod", "bada"], writes=["mod"])


def emit_ab(cx, ln_col, mod, sc_off, sh_off, A, B, key):
    S = cx.S
    if sc_off is None:
        S.op("dve", lambda e: e.tensor_copy(out=A[:], in_=ln_col[:]), reads=["lncols"], writes=[key])
        S.op("dve", lambda e: e.memset(B[:], 0.0), writes=[key], reads=[])
    else:
        S.op("dve", lambda e: e.scalar_tensor_tensor(out=A[:], in0=mod[:, sc_off:sc_off + 16], scalar=1.0, in1=ln_col[:], op0=ALU.add, op1=ALU.mult),
             reads=["mod", "lncols"], writes=[key])
        S.op("dve", lambda e: e.tensor_copy(out=B[:], in_=mod[:, sh_off:sh_off + 16]), reads=["mod"], writes=[key])


class Dense:
    def __init__(self, cx, TB):
        self.cx, self.TB = cx, TB
        self.NN = TB // 512
        S = cx.S
        self.ones = cx.sb([128, 128], F32, "ones")
        S.op("pool", lambda e: e.memset(self.ones[:], 1.0), writes=["ones"])
        self.hbuf = cx.sb([128, DC, TB], BF16, "hbuf")
        self.rr = cx.sb([128, TB], F32, "rr")
        self.sq = [cx.sb([128, 512], F32, "sq%d" % i) for i in range(2)]
        self.xc = [cx.sb([128, TB], F32, "xc%d" % i) for i in range(3)]
        self.wb = [cx.sb([128, 2048], BF16, "wb%d" % i) for i in range(4)]
        self.pss = [cx.ps([128, 512], F32, "pss%d" % i) for i in range(self.NN)]
        self.pmm = [cx.ps([128, 512], F32, "pmm%d" % i) for i in range(4)]
        self.ixc = 0
        self.iwb = 0
        self.ipm = 0
        self.isq = 0

    def next_xc(self):
        i = self.ixc % 3
        self.ixc += 1
        return i

    def load_w(self, src, ncols=2048):
        s = self.iwb % 4
        self.iwb += 1
        self.cx.S.dma("pool", self.wb[s][:, 0:ncols], src, "wb%d" % s, writes=["wb%d" % s])
        return s

    def next_pm(self):
        i = self.ipm % 4
        self.ipm += 1
        return i

    def acc_sumsq(self, xi, first, last):
        S = self.cx.S
        for n in range(self.NN):
            q = self.isq % 2
            self.isq += 1
            S.op("act", lambda e, q=q, n=n: e.activation(out=self.sq[q][:], in_=self.xc[xi][:, n * 512:(n + 1) * 512], func=AF.Square),
                 reads=["xc%d" % xi], writes=["sq%d" % q])
            S.op("pe", lambda e, q=q, n=n: e.matmul(self.pss[n][:], lhsT=self.ones[:], rhs=self.sq[q][:], start=first, stop=last),
                 reads=["ones", "sq%d" % q], writes=["pss%d" % n])

    def finish_rstd(self):
        S = self.cx.S
        for n in range(self.NN):
            S.op("act", lambda e, n=n: e.activation(out=self.rr[:, n * 512:(n + 1) * 512], in_=self.pss[n][:], func=AF.Sqrt, scale=1.0 / D, bias=EPS),
                 reads=["pss%d" % n], writes=["rr"])
        S.op("dve", lambda e: e.reciprocal(out=self.rr[:], in_=self.rr[:]), reads=["rr"], writes=["rr"])

    def sumsq_pass(self, x_dram, t0):
        S = self.cx.S
        for dc in range(DC):
            xi = self.next_xc()
            S.dma("sp", self.xc[xi][:], x_dram.ap[dc * 128:(dc + 1) * 128, t0:t0 + self.TB], "xc%d" % xi,
                  reads=[("xd", x_dram.name, dc, t0)], writes=["xc%d" % xi])
            self.acc_sumsq(xi, dc == 0, dc == DC - 1)

    def norm_pass(self, x_dram, t0, A, B, key, out_dram=None, out_dt=BF16):
        S = self.cx.S
        self.finish_rstd()
        for dc in range(DC):
            xi = self.next_xc()
            S.dma("sp", self.xc[xi][:], x_dram.ap[dc * 128:(dc + 1) * 128, t0:t0 + self.TB], "xc%d" % xi,
                  reads=[("xd", x_dram.name, dc, t0)], writes=["xc%d" % xi])
            S.op("dve", lambda e, xi=xi: e.tensor_tensor(out=self.xc[xi][:], in0=self.xc[xi][:], in1=self.rr[:], op=ALU.mult),
                 reads=["xc%d" % xi, "rr"], writes=["xc%d" % xi])
            if out_dram is not None and out_dt == F32:
                S.op("act", lambda e, xi=xi, dc=dc: e.activation(out=self.xc[xi][:], in_=self.xc[xi][:], func=AF.Identity,
                                                                   scale=A[:, dc:dc + 1], bias=B[:, dc:dc + 1]),
                     reads=["xc%d" % xi, key], writes=["xc%d" % xi])
                S.dma("sp", out_dram.ap[dc * 128:(dc + 1) * 128, t0:t0 + self.TB], self.xc[xi][:], ("ow", 3), reads=["xc%d" % xi],
                      writes=[("xd", out_dram.name, dc, t0)])
            else:
                S.op("act", lambda e, xi=xi, dc=dc: e.activation(out=self.hbuf[:, dc, :], in_=self.xc[xi][:], func=AF.Identity,
                                                                   scale=A[:, dc:dc + 1], bias=B[:, dc:dc + 1]),
                     reads=["xc%d" % xi, key], writes=[("h", dc)])
                if out_dram is not None:
                    S.dma("sp", out_dram.ap[dc * 128:(dc + 1) * 128, t0:t0 + self.TB], self.hbuf[:, dc, :], ("ow", 3), reads=[("h", dc)],
                          writes=[("xd", out_dram.name, dc, t0)])

    def proj_residual(self, w_t, nkc, rhs_fn, rhs_keys, x_in, x_out, t0, gate, goff):
        S = self.cx.S
        TB = self.TB
        for mc in range(DC):
            pieces = []
            k0 = 0
            while k0 < nkc:
                kn = min(16, nkc - k0)
                pieces.append((k0, kn, self.load_w(w_t[mc][:, k0 * 128:(k0 + kn) * 128], kn * 128)))
                k0 += kn
            xi = self.next_xc()
            S.dma("sp", self.xc[xi][:], x_in.ap[mc * 128:(mc + 1) * 128, t0:t0 + TB], "xc%d" % xi,
                  reads=[("xd", x_in.name, mc, t0)], writes=["xc%d" % xi])
            for n in range(self.NN):
                pb = self.next_pm()
                for (k0, kn, s) in pieces:
                    for kk in range(kn):
                        kc = k0 + kk
                        S.op("pe", lambda e, s=s, kk=kk, kc=kc, n=n, pb=pb: e.matmul(self.pmm[pb][:], lhsT=self.wb[s][:, kk * 128:(kk + 1) * 128],
                                                                                   rhs=rhs_fn(kc, n), start=(kc == 0), stop=(kc == nkc - 1)),
                             reads=["wb%d" % s, rhs_keys(kc, n)], writes=["pmm%d" % pb])
                S.op("dve", lambda e, n=n, pb=pb, xi=xi, mc=mc: e.scalar_tensor_tensor(
                    out=self.xc[xi][:, n * 512:(n + 1) * 512], in0=self.pmm[pb][:], scalar=gate[:, goff + mc:goff + mc + 1],
                    in1=self.xc[xi][:, n * 512:(n + 1) * 512], op0=ALU.mult, op1=ALU.add),
                    reads=["pmm%d" % pb, "xc%d" % xi, "mod"], writes=["xc%d" % xi])
            self.acc_sumsq(xi, mc == 0, mc == DC - 1)
            S.dma("sp", x_out.ap[mc * 128:(mc + 1) * 128, t0:t0 + TB], self.xc[xi][:], ("xw", 3), reads=["xc%d" % xi],
                  writes=[("xd", x_out.name, mc, t0)])

    def gate_up(self, wgu_t, act):
        S = self.cx.S
        if not hasattr(self, "gtmp"):
            self.gtmp = [self.cx.sb([128, 512], F32, "gtmp%d" % i) for i in range(2)]
            self.ig = 0
        for fc in range(FC):
            sg = self.load_w(wgu_t[fc])
            su = self.load_w(wgu_t[FC + fc])
            for n in range(self.NN):
                pg = self.next_pm()
                pu = self.next_pm()
                for (pb, s) in ((pg, sg), (pu, su)):
                    for kc in range(DC):
                        S.op("pe", lambda e, s=s, kc=kc, n=n, pb=pb: e.matmul(self.pmm[pb][:], lhsT=self.wb[s][:, kc * 128:(kc + 1) * 128],
                                                                            rhs=self.hbuf[:, kc, n * 512:(n + 1) * 512], start=(kc == 0), stop=(kc == DC - 1)),
                             reads=["wb%d" % s, ("h", kc)], writes=["pmm%d" % pb])
                g = self.ig % 2
                self.ig += 1
                S.op("act", lambda e, g=g, pg=pg: e.activation(out=self.gtmp[g][:], in_=self.pmm[pg][:], func=AF.Silu),
                     reads=["pmm%d" % pg], writes=["gtmp%d" % g])
                S.op("dve", lambda e, g=g, pu=pu, fc=fc, n=n: e.tensor_tensor(out=act[:, fc, n * 512:(n + 1) * 512], in0=self.gtmp[g][:], in1=self.pmm[pu][:], op=ALU.mult),
                     reads=["gtmp%d" % g, "pmm%d" % pu], writes=[("act", fc, n)])


def relayout_w(W, kdim_first=True):
    K, N = W.shape
    return np.ascontiguousarray(W.reshape(K // 128, 128, N // 128, 128).transpose(2, 1, 0, 3).reshape(N // 128, 128, K))


def col_layout(v):
    return np.ascontiguousarray(v.reshape(-1, 128).T)


class DR:
    def __init__(self, ap, name):
        self.ap, self.name = ap, name


SB = 512
BL = 128


class Rot:
    def __init__(self, tiles, name):
        self.t, self.name, self.i = tiles, name, 0

    def next(self):
        k = self.i % len(self.t)
        self.i += 1
        return self.t[k], "%s%d" % (self.name, k)


def emit_mixer(cx, T, hT, wfm, wtm, convw, hvec, nwcol, sinkv, cosT, sinT, consts, mixT):
    S, nc = cx.S, cx.nc
    NSB = T // SB
    sb, ps = cx.sb, cx.ps
    ident = sb([128, 128], F32, "ident")
    identb = sb([128, 128], BF16, "identb")
    triu = sb([128, 128], F32, "triu")
    ntril = sb([128, 128], F32, "ntril")
    Rm = sb([128, 128], F32, "Rm")
    amask = sb([128, 256], F32, "amask")
    amask0 = sb([128, 256], F32, "amask0")
    ones = sb([128, 128], F32, "ones")
    S.dma("sp", ident[:], consts[0], ("cst", 12), writes=["ident"])
    S.dma("sp", triu[:], consts[1], ("cst", 12), writes=["triu"])
    S.dma("sp", ntril[:], consts[2], ("cst", 12), writes=["ntril"])
    S.dma("sp", Rm[:], consts[3], ("cst", 12), writes=["Rm"])
    S.dma("sp", amask[:, 0:128], consts[4], ("cst", 12), writes=["amask"])
    S.dma("sp", amask[:, 128:256], consts[5], ("cst", 12), writes=["amask"])
    S.op("pool", lambda e: e.memset(ones[:], 1.0), writes=["ones"])
    S.op("dve", lambda e: e.tensor_copy(out=identb[:], in_=ident[:]), reads=["ident"], writes=["identb"])
    S.op("dve", lambda e: e.tensor_copy(out=amask0[:], in_=amask[:]), reads=["amask"], writes=["amask0"])
    S.op("dve", lambda e: e.memset(amask0[:, 0:128], -30000.0), reads=["amask0"], writes=["amask0"])
    wf = sb([128, 11, 2048], BF16, "wf")
    for b in range(11):
        S.dma("pool", wf[:, b, :], wfm[b], ("wfl", 12), writes=[("wf", b)])
    wt = sb([128, 16, 68], BF16, "wt")
    S.dma("pool", wt[:], wtm, ("wfl", 12), writes=["wt"])
    cw = sb([128, 24], F32, "cw")
    S.dma("sp", cw[:], convw, ("cst", 12), writes=["cw"])
    hv = sb([128, 4], F32, "hv")
    S.dma("sp", hv[:], hvec.to_broadcast([128, 4]), ("cst", 12), writes=["hv"])
    nw = sb([128, 1], F32, "nw")
    S.dma("sp", nw[:], nwcol, ("cst", 12), writes=["nw"])
    snk = sb([128, 4], F32, "snk")
    S.dma("sp", snk[:], sinkv.to_broadcast([128, 4]), ("cst", 12), writes=["snk"])
    nealog = sb([128, 2], F32, "nealog")
    S.op("act", lambda e: e.activation(out=nealog[:], in_=hv[:, 0:2], func=AF.Exp), reads=["hv"], writes=["nealog"])
    S.op("dve", lambda e: e.tensor_scalar(out=nealog[:], in0=nealog[:], scalar1=-1.0, scalar2=None, op0=ALU.mult), reads=["nealog"], writes=["nealog"])

    banks = Rot([ps([128, 512], F32, "bk%d" % i) for i in range(7)], "bk")

    class RV:
        def __init__(self, a, b):
            self.a, self.b = a, b

        def next(self):
            t, k = banks.next()
            return t[:, self.a:self.b], k
    pacc = RV(0, 512)
    pbig = RV(0, 512)
    phalf = RV(0, 256)
    pq = RV(0, 128)
    pq2 = RV(0, 128)
    ptb_t = ps([128, 2, 128], BF16, "ptb")
    ptb = Rot([ptb_t], "ptb")

    hsb = Rot([sb([128, DC, SB], BF16, "hsb%d" % i) for i in range(1)], "hsb")
    cin = [sb([128, 3 + SB], F32, "cin%d" % i) for i in range(6)]
    for i in range(6):
        S.op("pool", lambda e, i=i: e.memset(cin[i][:, 0:3], 0.0), writes=[("cin", i)])
    cact = [sb([128, SB], F32, "cact%d" % i) for i in range(6)]
    zs = [sb([128, SB], F32, "zs%d" % i) for i in range(2)]
    qraw = [sb([128, SB], F32, "qraw%d" % i) for i in range(3)]
    qat = [sb([128, SB], BF16, "qat%d" % i) for i in range(2)]
    kbuf = sb([128, BL + SB], BF16, "kbuf")
    S.op("pool", lambda e: e.memset(kbuf[:, 0:BL], 0.0), writes=["kbuf"])
    vbuf = sb([128, 5, 64], BF16, "vbuf")
    S.op("pool", lambda e: e.memset(vbuf[:, 0, :], 0.0), writes=[("vbuf", 0)])
    ab = sb([128, 4, 4], F32, "ab")
    cs_t = sb([128, SB], F32, "cs_t")
    sn_t = sb([128, SB], F32, "sn_t")
    tmpb = Rot([sb([128, SB], F32, "tmpb%d" % i) for i in range(2)], "tmpb")
    outdn = [sb([128, SB], BF16, "outdn%d" % i) for i in range(2)]
    outat = [sb([128, SB], BF16, "outat%d" % i) for i in range(2)]
    Sst = [sb([128, 128], F32, "Sst%d" % i) for i in range(2)]
    for h in range(2):
        S.op("pool", lambda e, h=h: e.memset(Sst[h][:], 0.0), writes=[("S", h)])
    NU = 8
    def ubuf(name, dt=F32, shape=(128, 128)):
        return [sb(list(shape), dt, "%s%d" % (name, u)) for u in range(NU)]
    sc = ubuf("sc", F32, (128, 8))
    E1 = ubuf("E1"); E2 = ubuf("E2"); Nn = ubuf("Nn"); MT = ubuf("MT"); Mm = ubuf("Mm"); PT = ubuf("PT")
    Kd = ubuf("Kd"); Kbg = ubuf("Kbg"); Vb = ubuf("Vb"); WT = ubuf("WT"); Uu = ubuf("Uu"); IT = ubuf("IT"); QgT = ubuf("QgT")
    gbb = ubuf("gbb")
    Vn = Rot([sb([128, 128], F32, "Vn%d" % i) for i in range(2)], "Vn")
    oT = Rot([sb([128, 128], F32, "oT%d" % i) for i in range(2)], "oT")
    sm = Rot([sb([128, 256], F32, "sm%d" % i) for i in range(2)], "sm")
    pe_ = Rot([sb([128, 256], F32, "pe%d" % i) for i in range(2)], "pe")
    pn = Rot([sb([128, 256], BF16, "pn%d" % i) for i in range(2)], "pn")
    pT = Rot([sb([128, 2, 128], BF16, "pT%d" % i) for i in range(2)], "pT")
    st = Rot([sb([128, 8], F32, "st%d" % i) for i in range(4)], "st")

    hTv = hT.rearrange("(kc p) t -> p kc t", p=128)

    for sbi in range(NSB):
        t0 = sbi * SB
        hs, hk = hsb.next()
        S.dma("sp", hs[:], hTv[:, :, t0:t0 + SB], hk, writes=[hk])
        S.dma("sp", cs_t[:], cosT[:, t0:t0 + SB], "cs", writes=["cs_t"])
        S.dma("sp", sn_t[:], sinT[:, t0:t0 + SB], "sn", writes=["sn_t"])
        for b in range(11):
            pa, pk = pacc.next()
            for kc in range(DC):
                S.op("pe", lambda e, pa=pa, b=b, kc=kc, hs=hs: e.matmul(pa[:], lhsT=wf[:, b, kc * 128:(kc + 1) * 128], rhs=hs[:, kc, :],
                                                                      start=(kc == 0), stop=(kc == DC - 1)), reads=[("wf", b), hk], writes=[pk])
            if b < 6:
                S.op("act", lambda e, pa=pa, b=b: e.copy(out=cin[b][:, 3:3 + SB], in_=pa[:]), reads=[pk], writes=[("cin", b)])
            elif b < 8:
                S.op("act", lambda e, pa=pa, b=b: e.activation(out=zs[b - 6][:], in_=pa[:], func=AF.Silu), reads=[pk], writes=[("zs", b - 6)])
            else:
                S.op("act", lambda e, pa=pa, b=b: e.copy(out=qraw[b - 8][:], in_=pa[:]), reads=[pk], writes=[("qraw", b - 8)])
        for j in range(4):
            pqt, pqk = pq.next()
            for kc in range(DC):
                S.op("pe", lambda e, pqt=pqt, kc=kc, j=j, hs=hs: e.matmul(pqt[:, 0:68], lhsT=hs[:, kc, j * BL:(j + 1) * BL], rhs=wt[:, kc, :],
                                                                        start=(kc == 0), stop=(kc == DC - 1)), reads=["wt", hk], writes=[pqk])
            S.op("act", lambda e, pqt=pqt, j=j: e.copy(out=vbuf[:, j + 1, :], in_=pqt[:, 0:64]), reads=[pqk], writes=[("vbuf", j + 1)])
            S.op("dve", lambda e, pqt=pqt, j=j: e.tensor_copy(out=ab[:, j, :], in_=pqt[:, 64:68]), reads=[pqk], writes=[("ab", j)])
        for r in range(3):
            pb_, pbk = pbig.next()
            S.op("pe", lambda e, pb_=pb_, r=r: e.matmul(pb_[:], lhsT=Rm[:], rhs=qraw[r][:], start=True, stop=True), reads=["Rm", ("qraw", r)], writes=[pbk])
            t1, t1k = tmpb.next()
            S.op("dve", lambda e, t1=t1, r=r: e.scalar_tensor_tensor(out=t1[:], in0=qraw[r][:], scalar=(0.125 if r < 2 else 1.0), in1=cs_t[:], op0=ALU.mult, op1=ALU.mult),
                 reads=[("qraw", r), "cs_t"], writes=[t1k])
            t2, t2k = tmpb.next()
            S.op("dve", lambda e, t2=t2, pb_=pb_, r=r: e.scalar_tensor_tensor(out=t2[:], in0=pb_[:], scalar=(0.125 if r < 2 else 1.0), in1=sn_t[:], op0=ALU.mult, op1=ALU.mult),
                 reads=[pbk, "sn_t"], writes=[t2k])
            if r < 2:
                S.op("dve", lambda e, t1=t1, t2=t2, r=r: e.tensor_tensor(out=qat[r][:], in0=t1[:], in1=t2[:], op=ALU.add), reads=[t1k, t2k], writes=[("qat", r)])
            else:
                S.op("dve", lambda e, t1=t1, t2=t2: e.tensor_tensor(out=kbuf[0:64, BL:BL + SB], in0=t1[0:64, :], in1=t2[0:64, :], op=ALU.add), reads=[t1k, t2k, "kbuf"], writes=["kbuf"])
                S.op("act", lambda e: e.copy(out=kbuf[64:128, BL:BL + SB], in_=kbuf[0:64, BL:BL + SB]), reads=["kbuf"], writes=["kbuf"])
        for b in range(6):
            t1, t1k = tmpb.next()
            S.op("dve", lambda e, t1=t1, b=b: e.tensor_scalar(out=t1[:], in0=cin[b][:, 0:SB], scalar1=cw[:, b * 4:b * 4 + 1], scalar2=None, op0=ALU.mult),
                 reads=[("cin", b), "cw"], writes=[t1k])
            for jj in range(1, 4):
                S.op("dve", lambda e, t1=t1, b=b, jj=jj: e.scalar_tensor_tensor(out=t1[:], in0=cin[b][:, jj:jj + SB], scalar=cw[:, b * 4 + jj:b * 4 + jj + 1], in1=t1[:],
                                                                              op0=ALU.mult, op1=ALU.add), reads=[("cin", b), "cw", t1k], writes=[t1k])
            S.op("act", lambda e, t1=t1, b=b: e.activation(out=cact[b][:], in_=t1[:], func=AF.Silu), reads=[t1k], writes=[("cact", b)])
            S.op("act", lambda e, b=b: e.copy(out=cin[b][:, 0:3], in_=cin[b][:, SB:SB + 3]), reads=[("cin", b)], writes=[("cin", b)])
        for b in range(4):
            t1, t1k = tmpb.next()
            S.op("act", lambda e, t1=t1, b=b: e.activation(out=t1[:], in_=cact[b][:], func=AF.Square), reads=[("cact", b)], writes=[t1k])
            pb_, pbk = pbig.next()
            S.op("pe", lambda e, pb_=pb_, t1=t1: e.matmul(pb_[:], lhsT=ones[:], rhs=t1[:], start=True, stop=True), reads=["ones", t1k], writes=[pbk])
            t2, t2k = tmpb.next()
            S.op("act", lambda e, t2=t2, pb_=pb_: e.activation(out=t2[:], in_=pb_[:], func=AF.Sqrt, bias=EPS, scale=1.0), reads=[pbk], writes=[t2k])
            S.op("dve", lambda e, t2=t2: e.reciprocal(out=t2[:], in_=t2[:]), reads=[t2k], writes=[t2k])
            S.op("dve", lambda e, t2=t2, b=b: e.scalar_tensor_tensor(out=cact[b][:], in0=cact[b][:], scalar=(128.0 ** -0.5 if b < 2 else 1.0), in1=t2[:], op0=ALU.mult, op1=ALU.mult),
                 reads=[("cact", b), t2k], writes=[("cact", b)])
        units = [(j, h) for j in range(4) for h in range(2)]
        def ukey(n, u):
            return (n, u)
        for (j, h) in units:
            u = j * 2 + h
            s_ = sc[u]
            k = ukey("sc", u)
            S.op("dve", lambda e, s_=s_, j=j, h=h: e.tensor_tensor(out=s_[:, 5:6], in0=ab[:, j, h:h + 1], in1=hv[:, 2 + h:3 + h], op=ALU.add), reads=[("ab", j), "hv"], writes=[k])
            S.op("act", lambda e, s_=s_: e.activation(out=s_[:, 5:6], in_=s_[:, 5:6], func=AF.Exp), reads=[k], writes=[k])
            S.op("act", lambda e, s_=s_: e.activation(out=s_[:, 5:6], in_=s_[:, 5:6], func=AF.Ln, bias=1.0, scale=1.0), reads=[k], writes=[k])
            S.op("dve", lambda e, s_=s_, h=h: e.tensor_tensor(out=s_[:, 1:2], in0=s_[:, 5:6], in1=nealog[:, h:h + 1], op=ALU.mult), reads=[k, "nealog"], writes=[k])
            S.op("act", lambda e, s_=s_, j=j, h=h: e.activation(out=s_[:, 0:1], in_=ab[:, j, 2 + h:3 + h], func=AF.Exp, scale=-1.0), reads=[("ab", j), k], writes=[k])
            S.op("dve", lambda e, s_=s_: e.tensor_scalar(out=s_[:, 0:1], in0=s_[:, 0:1], scalar1=1.0, scalar2=None, op0=ALU.add), reads=[k], writes=[k])
            S.op("dve", lambda e, s_=s_: e.reciprocal(out=s_[:, 0:1], in_=s_[:, 0:1]), reads=[k], writes=[k])
            p1, p1k = pq.next()
            S.op("pe", lambda e, p1=p1, s_=s_: e.matmul(p1[:, 0:2], lhsT=triu[:], rhs=s_[:, 0:2], start=True, stop=True), reads=["triu", k], writes=[p1k])
            S.op("pe", lambda e, p1=p1, s_=s_: e.matmul(p1[:, 2:4], lhsT=ones[:], rhs=s_[:, 0:2], start=True, stop=True), reads=["ones", k], writes=[p1k])
            S.op("dve", lambda e, p1=p1, s_=s_: e.tensor_copy(out=s_[:, 2:3], in_=p1[:, 1:2]), reads=[p1k, k], writes=[k])
            S.op("dve", lambda e, p1=p1, s_=s_: e.tensor_copy(out=s_[:, 7:8], in_=p1[:, 3:4]), reads=[p1k, k], writes=[k])
            S.op("act", lambda e, s_=s_: e.activation(out=s_[:, 6:7], in_=s_[:, 7:8], func=AF.Exp), reads=[k], writes=[k])
            S.op("act", lambda e, s_=s_: e.activation(out=s_[:, 3:4], in_=s_[:, 2:3], func=AF.Exp, scale=-1.0, bias=s_[:, 7:8]), reads=[k], writes=[k])
            S.op("act", lambda e, s_=s_: e.activation(out=s_[:, 4:5], in_=s_[:, 2:3], func=AF.Exp), reads=[k], writes=[k])
            S.op("dve", lambda e, s_=s_: e.tensor_tensor(out=s_[:, 4:5], in0=s_[:, 4:5], in1=s_[:, 0:1], op=ALU.mult), reads=[k], writes=[k])
            S.op("dve", lambda e, u=u, s_=s_: e.tensor_scalar(out=gbb[u][:], in0=ones[:], scalar1=s_[:, 1:2], scalar2=None, op0=ALU.mult), reads=["ones", k], writes=[ukey("gbb", u)])
            p2, p2k = pq.next()
            S.op("pe", lambda e, p2=p2, u=u: e.matmul(p2[:], lhsT=gbb[u][:], rhs=triu[:], start=True, stop=True), reads=[ukey("gbb", u), "triu"], writes=[p2k])
            S.op("dve", lambda e, p2=p2, u=u, s_=s_: e.tensor_scalar(out=E1[u][:], in0=p2[:], scalar1=s_[:, 2:3], scalar2=0.0, op0=ALU.subtract, op1=ALU.max), reads=[p2k, k], writes=[ukey("E1", u)])
            S.op("act", lambda e, u=u: e.activation(out=E1[u][:], in_=E1[u][:], func=AF.Exp, scale=-1.0), reads=[ukey("E1", u)], writes=[ukey("E1", u)])
            S.op("dve", lambda e, p2=p2, u=u, s_=s_: e.tensor_scalar(out=E2[u][:], in0=p2[:], scalar1=s_[:, 2:3], scalar2=0.0, op0=ALU.subtract, op1=ALU.min), reads=[p2k, k], writes=[ukey("E2", u)])
            S.op("act", lambda e, u=u: e.activation(out=E2[u][:], in_=E2[u][:], func=AF.Exp), reads=[ukey("E2", u)], writes=[ukey("E2", u)])
            S.op("pool", lambda e, u=u: e.tensor_tensor(out=E2[u][:], in0=E2[u][:], in1=triu[:], op=ALU.mult), reads=[ukey("E2", u), "triu"], writes=[ukey("E2", u)])
            S.op("act", lambda e, p2=p2, u=u: e.activation(out=QgT[u][:], in_=p2[:], func=AF.Exp), reads=[p2k], writes=[ukey("QgT", u)])
            S.op("dve", lambda e, u=u, j=j, h=h: e.tensor_tensor(out=QgT[u][:], in0=QgT[u][:], in1=cact[h][:, j * BL:(j + 1) * BL], op=ALU.mult), reads=[ukey("QgT", u), ("cact", h)], writes=[ukey("QgT", u)])
        for (j, h) in units:
            u = j * 2 + h
            s_ = sc[u]
            k = ukey("sc", u)
            kT = cact[2 + h][:, j * BL:(j + 1) * BL]
            qT = cact[h][:, j * BL:(j + 1) * BL]
            vT = cact[4 + h][:, j * BL:(j + 1) * BL]
            p1, p1k = pq.next()
            S.op("pe", lambda e, p1=p1, kT=kT: e.transpose(p1[:], kT, ident[:]), reads=[("cact", 2 + h), "ident"], writes=[p1k])
            S.op("dve", lambda e, p1=p1, u=u, s_=s_: e.tensor_scalar(out=Kd[u][:], in0=p1[:], scalar1=s_[:, 3:4], scalar2=None, op0=ALU.mult), reads=[p1k, k], writes=[ukey("Kd", u)])
            S.op("dve", lambda e, p1=p1, u=u, s_=s_: e.tensor_scalar(out=Kbg[u][:], in0=p1[:], scalar1=s_[:, 4:5], scalar2=None, op0=ALU.mult), reads=[p1k, k], writes=[ukey("Kbg", u)])
            p2, p2k = pq.next()
            S.op("pe", lambda e, p2=p2, vT=vT: e.transpose(p2[:], vT, ident[:]), reads=[("cact", 4 + h), "ident"], writes=[p2k])
            S.op("dve", lambda e, p2=p2, u=u, s_=s_: e.tensor_scalar(out=Vb[u][:], in0=p2[:], scalar1=s_[:, 0:1], scalar2=None, op0=ALU.mult), reads=[p2k, k], writes=[ukey("Vb", u)])
            p3, p3k = pq.next()
            S.op("pe", lambda e, p3=p3, kT=kT: e.matmul(p3[:], lhsT=kT, rhs=kT, start=True, stop=True), reads=[("cact", 2 + h)], writes=[p3k])
            S.op("dve", lambda e, p3=p3, u=u: e.tensor_tensor(out=Nn[u][:], in0=p3[:], in1=E1[u][:], op=ALU.mult), reads=[p3k, ukey("E1", u)], writes=[ukey("Nn", u)])
            S.op("dve", lambda e, u=u, s_=s_: e.scalar_tensor_tensor(out=Nn[u][:], in0=Nn[u][:], scalar=s_[:, 0:1], in1=ntril[:], op0=ALU.mult, op1=ALU.mult), reads=[ukey("Nn", u), k, "ntril"], writes=[ukey("Nn", u)])
            p4, p4k = pq.next()
            S.op("pe", lambda e, p4=p4, kT=kT, qT=qT: e.matmul(p4[:], lhsT=kT, rhs=qT, start=True, stop=True), reads=[("cact", 2 + h), ("cact", h)], writes=[p4k])
            S.op("dve", lambda e, p4=p4, u=u: e.tensor_tensor(out=IT[u][:], in0=p4[:], in1=E2[u][:], op=ALU.mult), reads=[p4k, ukey("E2", u)], writes=[ukey("IT", u)])
            p5, p5k = pq.next()
            S.op("pe", lambda e, p5=p5, u=u: e.transpose(p5[:], Nn[u][:], ident[:]), reads=[ukey("Nn", u), "ident"], writes=[p5k])
            S.op("act", lambda e, p5=p5, u=u: e.copy(out=MT[u][:], in_=p5[:]), reads=[p5k], writes=[ukey("MT", u)])
            S.op("dve", lambda e, p5=p5, u=u: e.tensor_tensor(out=PT[u][:], in0=p5[:], in1=ident[:], op=ALU.add), reads=[p5k, "ident"], writes=[ukey("PT", u)])
            S.op("pool", lambda e, u=u: e.tensor_copy(out=Mm[u][:], in_=Nn[u][:]), reads=[ukey("Nn", u)], writes=[ukey("Mm", u)])
        for it in range(1, 7):
            for (j, h) in units:
                u = j * 2 + h
                pa_, pak = pq.next()
                S.op("pe", lambda e, pa_=pa_, u=u: e.matmul(pa_[:], lhsT=MT[u][:], rhs=Mm[u][:], start=True, stop=True), reads=[ukey("MT", u), ukey("Mm", u)], writes=[pak])
                if it < 6:
                    pb2, pb2k = pq.next()
                    S.op("pe", lambda e, pb2=pb2, u=u: e.matmul(pb2[:], lhsT=Mm[u][:], rhs=MT[u][:], start=True, stop=True), reads=[ukey("MT", u), ukey("Mm", u)], writes=[pb2k])
                S.op("act", lambda e, pa_=pa_, u=u: e.copy(out=Mm[u][:], in_=pa_[:]), reads=[pak], writes=[ukey("Mm", u)])
                if it < 6:
                    S.op("dve", lambda e, pb2=pb2, u=u: e.tensor_copy(out=MT[u][:], in_=pb2[:]), reads=[pb2k], writes=[ukey("MT", u)])
                pc_, pck = pq.next()
                S.op("pe", lambda e, pc_=pc_, u=u: e.matmul(pc_[:], lhsT=Mm[u][:], rhs=PT[u][:], start=True, stop=True), reads=[ukey("Mm", u), ukey("PT", u)], writes=[pck])
                S.op("dve", lambda e, pc_=pc_, u=u: e.tensor_tensor(out=PT[u][:], in0=PT[u][:], in1=pc_[:], op=ALU.add), reads=[pck, ukey("PT", u)], writes=[ukey("PT", u)])
        for (j, h) in units:
            u = j * 2 + h
            p1, p1k = pq.next()
            S.op("pe", lambda e, p1=p1, u=u: e.matmul(p1[:], lhsT=Kbg[u][:], rhs=PT[u][:], start=True, stop=True), reads=[ukey("Kbg", u), ukey("PT", u)], writes=[p1k])
            S.op("act", lambda e, p1=p1, u=u: e.copy(out=WT[u][:], in_=p1[:]), reads=[p1k], writes=[ukey("WT", u)])
            p2, p2k = pq.next()
            S.op("pe", lambda e, p2=p2, u=u: e.matmul(p2[:], lhsT=PT[u][:], rhs=Vb[u][:], start=True, stop=True), reads=[ukey("Vb", u), ukey("PT", u)], writes=[p2k])
            S.op("dve", lambda e, p2=p2, u=u: e.tensor_copy(out=Uu[u][:], in_=p2[:]), reads=[p2k], writes=[ukey("Uu", u)])
        for (j, h) in units:
            u = j * 2 + h
            s_ = sc[u]
            k = ukey("sc", u)
            p1, p1k = pq.next()
            S.op("pe", lambda e, p1=p1, u=u, h=h: e.matmul(p1[:], lhsT=WT[u][:], rhs=Sst[h][:], start=True, stop=True), reads=[ukey("WT", u), ("S", h)], writes=[p1k])
            vn, vnk = Vn.next()
            S.op("dve", lambda e, p1=p1, vn=vn, u=u: e.tensor_tensor(out=vn[:], in0=Uu[u][:], in1=p1[:], op=ALU.subtract), reads=[p1k, ukey("Uu", u)], writes=[vnk])
            po, pok = pq2.next()
            S.op("pe", lambda e, po=po, u=u, h=h: e.matmul(po[:], lhsT=Sst[h][:], rhs=QgT[u][:], start=True, stop=False), reads=[("S", h), ukey("QgT", u)], writes=[pok])
            S.op("pe", lambda e, po=po, u=u, vn=vn: e.matmul(po[:], lhsT=vn[:], rhs=IT[u][:], start=False, stop=True), reads=[vnk, ukey("IT", u)], writes=[pok])
            p3, p3k = pq.next()
            S.op("pe", lambda e, p3=p3, u=u, vn=vn: e.matmul(p3[:], lhsT=Kd[u][:], rhs=vn[:], start=True, stop=True), reads=[vnk, ukey("Kd", u)], writes=[p3k])
            S.op("dve", lambda e, p3=p3, h=h, s_=s_: e.scalar_tensor_tensor(out=Sst[h][:], in0=Sst[h][:], scalar=s_[:, 6:7], in1=p3[:], op0=ALU.mult, op1=ALU.add), reads=[p3k, ("S", h), k], writes=[("S", h)])
            o_, ok_ = oT.next()
            S.op("act", lambda e, po=po, o_=o_: e.copy(out=o_[:], in_=po[:]), reads=[pok], writes=[ok_])
            t1, t1k = tmpb.next()
            S.op("act", lambda e, t1=t1, o_=o_: e.activation(out=t1[:, 0:128], in_=o_[:], func=AF.Square), reads=[ok_], writes=[t1k])
            p4, p4k = pq.next()
            S.op("pe", lambda e, p4=p4, t1=t1: e.matmul(p4[:], lhsT=ones[:], rhs=t1[:, 0:128], start=True, stop=True), reads=["ones", t1k], writes=[p4k])
            S.op("act", lambda e, p4=p4, t1=t1: e.activation(out=t1[:, 128:256], in_=p4[:], func=AF.Sqrt, scale=1.0 / 128, bias=EPS), reads=[p4k, t1k], writes=[t1k])
            S.op("dve", lambda e, t1=t1: e.reciprocal(out=t1[:, 128:256], in_=t1[:, 128:256]), reads=[t1k], writes=[t1k])
            S.op("dve", lambda e, t1=t1, o_=o_: e.tensor_tensor(out=o_[:], in0=o_[:], in1=t1[:, 128:256], op=ALU.mult), reads=[t1k, ok_], writes=[ok_])
            S.op("dve", lambda e, o_=o_, h=h, j=j: e.scalar_tensor_tensor(out=outdn[h][:, j * BL:(j + 1) * BL], in0=o_[:], scalar=nw[:, 0:1], in1=zs[h][:, j * BL:(j + 1) * BL], op0=ALU.mult, op1=ALU.mult),
                 reads=[ok_, "nw", ("zs", h)], writes=[("outdn", h)])
        for h in range(2):
            S.dma("sp", mixT[h * 128:(h + 1) * 128, t0:t0 + SB], outdn[h][:], ("mo", 4), reads=[("outdn", h)])
        for j in range(4):
            first = (sbi == 0 and j == 0)
            am = amask0 if first else amask
            amk = "amask0" if first else "amask"
            for hq in range(4):
                blk, ro = hq // 2, (hq % 2) * 64
                psc, psk = phalf.next()
                S.op("pe", lambda e, psc=psc, blk=blk, ro=ro, j=j: e.matmul(psc[:], lhsT=qat[blk][ro:ro + 64, j * BL:(j + 1) * BL], rhs=kbuf[ro:ro + 64, j * BL:j * BL + 256],
                                                                          start=True, stop=True), reads=[("qat", blk), "kbuf"], writes=[psk])
                sm_, smk = sm.next()
                S.op("dve", lambda e, psc=psc, sm_=sm_, am=am: e.tensor_tensor(out=sm_[:], in0=psc[:], in1=am[:], op=ALU.add), reads=[psk, amk], writes=[smk])
                st_, stk = st.next()
                S.op("dve", lambda e, sm_=sm_, st_=st_: e.reduce_max(out=st_[:, 0:1], in_=sm_[:], axis=mybir.AxisListType.X), reads=[smk], writes=[stk])
                S.op("dve", lambda e, st_=st_, hq=hq: e.tensor_tensor(out=st_[:, 0:1], in0=st_[:, 0:1], in1=snk[:, hq:hq + 1], op=ALU.max), reads=[stk, "snk"], writes=[stk])
                S.op("dve", lambda e, st_=st_: e.tensor_scalar(out=st_[:, 1:2], in0=st_[:, 0:1], scalar1=-1.0, scalar2=None, op0=ALU.mult), reads=[stk], writes=[stk])
                pe2, pek = pe_.next()
                S.op("act", lambda e, pe2=pe2, sm_=sm_, st_=st_: e.activation(out=pe2[:], in_=sm_[:], func=AF.Exp, bias=st_[:, 1:2], scale=1.0, accum_out=st_[:, 2:3]), reads=[smk, stk], writes=[pek, stk])
                S.op("act", lambda e, st_=st_, hq=hq: e.activation(out=st_[:, 3:4], in_=snk[:, hq:hq + 1], func=AF.Exp, bias=st_[:, 1:2], scale=1.0), reads=[stk, "snk"], writes=[stk])
                S.op("dve", lambda e, st_=st_: e.tensor_tensor(out=st_[:, 3:4], in0=st_[:, 3:4], in1=st_[:, 2:3], op=ALU.add), reads=[stk], writes=[stk])
                S.op("dve", lambda e, st_=st_: e.reciprocal(out=st_[:, 3:4], in_=st_[:, 3:4]), reads=[stk], writes=[stk])
                pn_, pnk = pn.next()
                S.op("dve", lambda e, pn_=pn_, pe2=pe2, st_=st_: e.tensor_scalar(out=pn_[:], in0=pe2[:], scalar1=st_[:, 3:4], scalar2=None, op0=ALU.mult), reads=[pek, stk], writes=[pnk])
                pt_, ptk = ptb.next()
                for c in range(2):
                    S.op("pe", lambda e, pt_=pt_, pn_=pn_, c=c: e.transpose(pt_[:, c, :], pn_[:, c * 128:(c + 1) * 128], identb[:]), reads=[pnk, "identb"], writes=[ptk])
                pT_, pTk = pT.next()
                S.op("act", lambda e, pt_=pt_, pT_=pT_: e.copy(out=pT_[:], in_=pt_[:]), reads=[ptk], writes=[pTk])
                po, pok = pq2.next()
                for c in range(2):
                    S.op("pe", lambda e, po=po, pT_=pT_, c=c, j=j: e.matmul(po[0:64, :], lhsT=vbuf[:, j + c, :], rhs=pT_[:, c, :], start=(c == 0), stop=(c == 1)),
                         reads=[("vbuf", j + c), pTk], writes=[pok])
                S.op("act", lambda e, po=po, blk=blk, ro=ro, j=j: e.copy(out=outat[blk][ro:ro + 64, j * BL:(j + 1) * BL], in_=po[0:64, :]), reads=[pok], writes=[("outat", blk)])
        for blk in range(2):
            S.dma("sp", mixT[256 + blk * 128:256 + (blk + 1) * 128, t0:t0 + SB], outat[blk][:], ("mo", 4), reads=[("outat", blk)])
        S.op("act", lambda e: e.copy(out=kbuf[:, 0:BL], in_=kbuf[:, SB:SB + BL]), reads=["kbuf"], writes=["kbuf"])
        S.op("dve", lambda e: e.tensor_copy(out=vbuf[:, 0, :], in_=vbuf[:, 4, :]), reads=[("vbuf", 4)], writes=[("vbuf", 0)])
    return S.chan_tokens("mo")


def mixer_consts():
    i = np.arange(128)
    ident = np.eye(128, dtype=np.float32)
    triu = (i[None, :] >= i[:, None]).astype(np.float32)
    ntril = -(i[None, :] < i[:, None]).astype(np.float32)
    Rm = np.zeros((128, 128), np.float32)
    for m in range(128):
        if (m % 64) < 32:
            Rm[m + 32, m] = -1.0
        else:
            Rm[m - 32, m] = 1.0
    r = i[:, None]
    jj = np.arange(256)[None, :]
    vis = (jj > r) & (jj <= r + 128)
    am = np.where(vis, 0.0, -30000.0).astype(np.float32)
    return np.stack([ident, triu, ntril, Rm, am[:, :128], am[:, 128:]])


def rope_tables(T):
    half = 32
    inv = 10000.0 ** (-np.arange(half, dtype=np.float32) * 2.0 / 64)
    ang = np.arange(T, dtype=np.float32)[None, :] * inv[:, None]
    c = np.cos(ang).astype(np.float32)
    s = np.sin(ang).astype(np.float32)
    return np.ascontiguousarray(np.concatenate([c] * 4, axis=0)), np.ascontiguousarray(np.concatenate([s] * 4, axis=0))


def build_mod(NJ):
    nc = bass.Bass("TRN2", target_bir_lowering=False)
    c2 = nc.dram_tensor("c2", [128, 16, 2], F32, kind="ExternalInput").ap()
    wada_s = nc.dram_tensor("wada_s", [NJ, 128, 2048], F32, kind="ExternalInput").ap()
    bada_s = nc.dram_tensor("bada_s", [128, NJ], F32, kind="ExternalInput").ap()
    mod_o = nc.dram_tensor("mod_o", [128, NJ, 2], F32, kind="ExternalOutput").ap()
    with ExitStack() as st:
        S = Sched(nc, st)
        cx = Ctx(nc, st, S)
        cc = cx.sb([128, DC, 2], F32, "cc")
        cact = cx.sb([128, DC, 2], F32, "cact")
        bada = cx.sb([128, NJ], F32, "bada")
        modt = cx.sb([128, NJ, 2], F32, "modt")
        slabs = [cx.sb([128, 2048], F32, "wa%d" % i) for i in range(3)]
        pm = cx.ps([128, 512], F32, "pmod")
        block = st.enter_context(nc.Block())
        S.dma("sp", cc[:], c2, "misc", writes=["cc"])
        S.dma("sp", bada[:], bada_s, "misc", writes=["bada"])
        S.op("act", lambda e: e.activation(out=cact[:], in_=cc[:], func=AF.Silu), reads=["cc"], writes=["cact"])
        for j in range(NJ):
            s = j % 3
            S.dma("sp", slabs[s][:], wada_s[j], "wa%d" % s, writes=["wa%d" % s])
            for kc in range(DC):
                S.op("pe", lambda e, s=s, kc=kc, j=j: e.matmul(pm[:, 2 * j:2 * j + 2], lhsT=slabs[s][:, kc * 128:(kc + 1) * 128],
                                                                 rhs=cact[:, kc, :], start=(kc == 0), stop=(kc == DC - 1)),
                     reads=["wa%d" % s, "cact"], writes=["pmod"])
        pv = pm[:, 0:2 * NJ].rearrange("p (j t) -> p j t", t=2)
        for bb in range(2):
            S.op("dve", lambda e, bb=bb: e.tensor_tensor(out=modt[:, :, bb], in0=pv[:, :, bb], in1=bada[:], op=ALU.add),
                 reads=["pmod", "bada"], writes=["modt"])
        S.dma("sp", mod_o, modt[:], ("ow", 3), reads=["modt"])
        S.emit(block, final_tokens=S.chan_tokens("ow"))
    return nc


def build_A(NT, TB):
    nc = bass.Bass("TRN2", target_bir_lowering=False)
    mod_i = nc.dram_tensor("mod_i", [128, 192], F32, kind="ExternalInput").ap()
    lnm = nc.dram_tensor("lnm", [128, 16], F32, kind="ExternalInput").ap()
    xT = DR(nc.dram_tensor("xT", [D, NT], F32, kind="ExternalInput").ap(), "xT")
    hT = DR(nc.dram_tensor("hT", [D, NT], BF16, kind="ExternalOutput").ap(), "hT")
    with ExitStack() as st:
        S = Sched(nc, st)
        cx = Ctx(nc, st, S)
        mod = cx.sb([128, 192], F32, "mod")
        lnc = cx.sb([128, 16], F32, "lnc")
        A = cx.sb([128, 16], F32, "A")
        B = cx.sb([128, 16], F32, "B")
        block = st.enter_context(nc.Block())
        S.dma("sp", lnc[:], lnm, "misc", writes=["lncols"])
        S.dma("sp", mod[:], mod_i, "misc", writes=["mod"])
        emit_ab(cx, lnc, mod, 16, 0, A, B, "AB")
        dn = Dense(cx, TB)
        for tb in range(NT // TB):
            dn.sumsq_pass(xT, tb * TB)
            dn.norm_pass(xT, tb * TB, A, B, "AB", out_dram=hT, out_dt=BF16)
        S.emit(block, final_tokens=S.chan_tokens("ow"))
    return nc


def build_C(NT, TB, l, last):
    nc = bass.Bass("TRN2", target_bir_lowering=False)
    xT = DR(nc.dram_tensor("xT", [D, NT], F32, kind="ExternalInput").ap(), "xT")
    mixT = nc.dram_tensor("mixT", [D, NT], BF16, kind="ExternalInput").ap()
    mod_i = nc.dram_tensor("mod_i", [128, 192], F32, kind="ExternalInput").ap()
    lnf = nc.dram_tensor("lnf", [128, 16], F32, kind="ExternalInput").ap()
    lnn = nc.dram_tensor("lnn", [128, 16], F32, kind="ExternalInput").ap()
    wout_t = nc.dram_tensor("wout_t", [16, 128, 2048], F32, kind="ExternalInput").ap()
    wgu_t = nc.dram_tensor("wgu_t", [88, 128, 2048], F32, kind="ExternalInput").ap()
    wdn_t = nc.dram_tensor("wdn_t", [16, 128, FFN], F32, kind="ExternalInput").ap()
    xmid = DR(nc.dram_tensor("xmid", [D, NT], F32).ap(), "xmid")
    xfin = DR(nc.dram_tensor("xfin", [D, NT], F32, kind="ExternalOutput").ap(), "xfin")
    if last:
        outp = DR(nc.dram_tensor("outT", [D, NT], F32, kind="ExternalOutput").ap(), "outp")
    else:
        outp = DR(nc.dram_tensor("hT", [D, NT], BF16, kind="ExternalOutput").ap(), "outp")
    with ExitStack() as st:
        S = Sched(nc, st)
        cx = Ctx(nc, st, S)
        mod = cx.sb([128, 192], F32, "mod")
        lnc = cx.sb([128, 32], F32, "lnc")
        A1 = cx.sb([128, 16], F32, "A1")
        B1 = cx.sb([128, 16], F32, "B1")
        A2 = cx.sb([128, 16], F32, "A2")
        B2 = cx.sb([128, 16], F32, "B2")
        block = st.enter_context(nc.Block())
        S.dma("sp", mod[:], mod_i, "misc", writes=["mod"])
        S.dma("sp", lnc[:, 0:16], lnf, "misc", writes=["lncols"])
        S.dma("sp", lnc[:, 16:32], lnn, "misc", writes=["lncols"])
        o = l * 96
        emit_ab(cx, lnc[:, 0:16], mod, o + 64, o + 48, A1, B1, "AB1")
        if last:
            emit_ab(cx, lnc[:, 16:32], mod, None, None, A2, B2, "AB2")
        else:
            emit_ab(cx, lnc[:, 16:32], mod, o + 96 + 16, o + 96 + 0, A2, B2, "AB2")
        dn = Dense(cx, TB)
        act = cx.sb([128, FC, TB], BF16, "act")
        mixv = mixT.rearrange("(kc p) t -> p kc t", p=128)
        for tb in range(NT // TB):
            t0 = tb * TB
            S.dma("sp", dn.hbuf[:], mixv[:, :, t0:t0 + TB], "hload", writes=[("h", kc) for kc in range(DC)])
            dn.proj_residual(wout_t, 16, lambda kc, n: dn.hbuf[:, kc, n * 512:(n + 1) * 512], lambda kc, n: ("h", kc), xT, xmid, t0, mod, o + 32)
            dn.norm_pass(xmid, t0, A1, B1, "AB1")
            dn.gate_up(wgu_t, act)
            dn.proj_residual(wdn_t, FC, lambda kc, n: act[:, kc, n * 512:(n + 1) * 512], lambda kc, n: ("act", kc, n), xmid, xfin, t0, mod, o + 80)
            dn.norm_pass(xfin, t0, A2, B2, "AB2", out_dram=outp, out_dt=(F32 if last else BF16))
        S.emit(block, final_tokens=S.chan_tokens("ow") + S.chan_tokens("xw"))
    return nc

def mixer_inputs(w_in, conv_w, a_log, dt_bias, norm_w, sinks, g):
    DNW = 1024
    cols = []
    for part in range(3):
        for h in range(2):
            c0 = part * DNW + (2 * g + h) * 128
            cols.append(np.arange(c0, c0 + 128))
    for h in range(2):
        c0 = 3 * DNW + (2 * g + h) * 128
        cols.append(np.arange(c0, c0 + 128))
    atq0 = 4 * DNW + 16
    for blk in range(2):
        c0 = atq0 + (4 * g + 2 * blk) * 64
        cols.append(np.arange(c0, c0 + 128))
    kvh = g // 2
    k0 = atq0 + 1024 + kvh * 64
    v0 = atq0 + 1024 + 128 + kvh * 64
    cols.append(np.concatenate([np.arange(k0, k0 + 64), np.arange(v0, v0 + 64)]))
    wfm = np.stack([relayout_w(np.ascontiguousarray(w_in[:, c]))[0] for c in cols])
    b0 = 4 * DNW
    tmc = np.concatenate([np.arange(v0, v0 + 64), [b0 + 8 + 2 * g, b0 + 8 + 2 * g + 1], [b0 + 2 * g, b0 + 2 * g + 1]])
    wtm = np.ascontiguousarray(w_in[:, tmc].reshape(16, 128, 68).transpose(1, 0, 2))
    convw = np.zeros((128, 24), np.float32)
    for b in range(6):
        part, h = b // 2, b % 2
        ch = part * DNW + (2 * g + h) * 128 + np.arange(128)
        convw[:, b * 4:(b + 1) * 4] = conv_w[:, ch].T
    hvec = np.array([[a_log[2 * g], a_log[2 * g + 1], dt_bias[2 * g], dt_bias[2 * g + 1]]], np.float32)
    return dict(wfm=wfm, wtm=wtm, convw=convw, hvec=hvec, nwcol=np.ascontiguousarray(norm_w.reshape(128, 1)),
                sinkv=np.ascontiguousarray(sinks[4 * g:4 * g + 4].reshape(1, 4)))


def build_B(T):
    nc = bass.Bass("TRN2", target_bir_lowering=False)
    di = lambda n, s, d=F32: nc.dram_tensor(n, s, d, kind="ExternalInput").ap()
    hT = di("hT", [2048, T], BF16)
    wfm = di("wfm", [11, 128, 2048]); wtm = di("wtm", [128, 16, 68]); convw = di("convw", [128, 24])
    hvec = di("hvec", [1, 4]); nwcol = di("nwcol", [128, 1]); sinkv = di("sinkv", [1, 4])
    cosT = di("cosT", [128, T]); sinT = di("sinT", [128, T]); consts = di("consts", [6, 128, 128])
    mixT = nc.dram_tensor("mixT", [512, T], BF16, kind="ExternalOutput").ap()
    with ExitStack() as st:
        S = Sched(nc, st)
        cx = Ctx(nc, st, S)
        block = st.enter_context(nc.Block())
        tok = emit_mixer(cx, T, hT, wfm, wtm, convw, hvec, nwcol, sinkv, cosT, sinT, consts, mixT)
        S.emit(block, final_tokens=tok)
    return nc


IN_COLS = 5392


def kernel(x, c, ln_mix, ln_ffn, w_ada, b_ada, w_in, dn_conv_w, dn_a_log, dn_dt_bias, dn_norm_w, attn_sinks,
           w_out, w_gate_up, w_down, ln_final):
    f32 = lambda a: np.ascontiguousarray(np.asarray(a, dtype=np.float32))
    x, c, ln_mix, ln_ffn, w_ada, b_ada, w_in = map(f32, (x, c, ln_mix, ln_ffn, w_ada, b_ada, w_in))
    dn_conv_w, dn_a_log, dn_dt_bias, dn_norm_w, attn_sinks = map(f32, (dn_conv_w, dn_a_log, dn_dt_bias, dn_norm_w, attn_sinks))
    w_out, w_gate_up, w_down, ln_final = map(f32, (w_out, w_gate_up, w_down, ln_final))
    NB, T, Dm = x.shape
    NG = 4
    NT = T // NG
    TB = 1024
    cores = list(range(NB * NG))
    NJc = 12
    c2 = np.ascontiguousarray(np.stack([col_layout(c[0]), col_layout(c[1])], axis=2))
    wt_l = [relayout_w(w_ada[l]) for l in range(2)]
    bc_l = [col_layout(b_ada[l]) for l in range(2)]
    inM = []
    for r in cores:
        sl = slice(r * NJc, (r + 1) * NJc)
        inM.append({"c2": c2, "wada_s": np.ascontiguousarray(np.concatenate([wt_l[0][sl], wt_l[1][sl]])),
                    "bada_s": np.ascontiguousarray(np.concatenate([bc_l[0][:, sl], bc_l[1][:, sl]], axis=1))})
    resM = run_bass_kernel_spmd(build_mod(2 * NJc), inM, core_ids=cores).results
    del wt_l, inM
    modb = [np.empty((128, 192), np.float32) for _ in range(NB)]
    for r in cores:
        mo = resM[r]["mod_o"]
        for l in range(2):
            for bb in range(NB):
                modb[bb][:, l * 96 + r * NJc:l * 96 + (r + 1) * NJc] = mo[:, l * NJc:(l + 1) * NJc, bb]
    mod = [modb[r // NG] for r in cores]
    xT = [np.ascontiguousarray(x[r // NG, (r % NG) * NT:(r % NG + 1) * NT, :].T) for r in cores]
    resA = run_bass_kernel_spmd(build_A(NT, TB), [{"mod_i": mod[r], "lnm": col_layout(ln_mix[0]), "xT": xT[r]} for r in cores], core_ids=cores).results
    hT = [resA[r]["hT"] for r in cores]
    cs, sn = rope_tables(T)
    consts = mixer_consts()
    ncB = build_B(T)
    out = None
    for l in range(2):
        last = l == 1
        hfull = [np.ascontiguousarray(np.concatenate([hT[b * NG + g] for g in range(NG)], axis=1)) for b in range(NB)]
        inB = []
        for r in cores:
            d = mixer_inputs(w_in[l], dn_conv_w[l], dn_a_log[l], dn_dt_bias[l], dn_norm_w[l], attn_sinks[l], r % NG)
            d.update(hT=hfull[r // NG], cosT=cs, sinT=sn, consts=consts)
            inB.append(d)
        resB = run_bass_kernel_spmd(ncB, inB, core_ids=cores).results
        del hfull, inB
        mixfull = []
        for b in range(NB):
            dn = np.concatenate([resB[b * NG + g]["mixT"][0:256] for g in range(NG)], axis=0)
            at = np.concatenate([resB[b * NG + g]["mixT"][256:512] for g in range(NG)], axis=0)
            mixfull.append(np.concatenate([dn, at], axis=0))
        wout_t = relayout_w(w_out[l])
        wgu_t = relayout_w(w_gate_up[l])
        wdn_t = relayout_w(w_down[l])
        lnn = col_layout(ln_final if last else ln_mix[l + 1])
        ncC = build_C(NT, TB, l, last)
        inC = [{"xT": xT[r], "mixT": np.ascontiguousarray(mixfull[r // NG][:, (r % NG) * NT:(r % NG + 1) * NT]), "mod_i": mod[r],
                "lnf": col_layout(ln_ffn[l]), "lnn": lnn, "wout_t": wout_t, "wgu_t": wgu_t, "wdn_t": wdn_t} for r in cores]
        resC = run_bass_kernel_spmd(ncC, inC, core_ids=cores).results
        del inC, wout_t, wgu_t, wdn_t, mixfull
        xT = [resC[r]["xfin"] for r in cores]
        if last:
            out = np.empty((NB, T, Dm), np.float32)
            for r in cores:
                out[r // NG, (r % NG) * NT:(r % NG + 1) * NT, :] = resC[r]["outT"].T
        else:
            hT = [resC[r]["hT"] for r in cores]
    return out
```

```python
import numpy as np
import ml_dtypes
from contextlib import ExitStack
import concourse.bass as bass
import concourse.mybir as mybir
from concourse.bass_utils import run_bass_kernel_spmd


ENGS = ("pe", "act", "dve", "pool", "sp")


class _Op:
    __slots__ = ("eng", "fn", "deps", "chan", "marked", "val")

    def __init__(self, eng, fn, deps, chan=None):
        self.eng, self.fn, self.deps, self.chan = eng, fn, deps, chan
        self.marked = False
        self.val = None


class Sched:
    def __init__(self, nc, stack):
        self.nc, self.stack = nc, stack
        self.ops = {e: [] for e in ENGS}
        self.psem = {e: stack.enter_context(nc.semaphore("prog_" + e)) for e in ENGS[:4]}
        self.csem, self.ccnt = {}, {}
        self.last_w, self.readers = {}, {}
        self.out_tokens = []
        self.rot = {}

    def _deps(self, reads, writes):
        d = []
        for r in reads:
            t = self.last_w.get(r)
            if t is not None:
                d.append(t)
        for w in writes:
            t = self.last_w.get(w)
            if t is not None:
                d.append(t)
            d.extend(self.readers.get(w, ()))
        return [("c", t[1], self.ccnt[t[1]]) if t[0] == "c" else t for t in d]

    def chan_tokens(self, base):
        return [("c", ch, n) for ch, n in self.ccnt.items() if ch == base or ch.startswith(base + "#")]

    def barrier(self):
        toks = []
        for e in ENGS[:4]:
            idx = [i for i, o in enumerate(self.ops[e]) if o.fn is not None and o.chan is None]
            if idx:
                toks.append(("e", e, idx[-1]))
        toks += [("c", ch, n) for ch, n in self.ccnt.items() if n]
        for e in ENGS:
            self.ops[e].append(_Op(e, None, list(toks)))

    def _record(self, tok, reads, writes):
        for r in reads:
            self.readers.setdefault(r, []).append(tok)
        for w in writes:
            self.last_w[w] = tok
            self.readers[w] = []

    PSUM_PREFIX = ("bk", "ptb", "pmm", "pss", "pmod")

    def op(self, eng, fn, reads=(), writes=()):
        ex = [r for r in reads if isinstance(r, str) and r.startswith(self.PSUM_PREFIX)]
        if ex:
            reads = [r for r in reads if r not in ex]
            writes = list(writes) + ex
        deps = self._deps(reads, writes)
        if eng == "pe":
            deps = [t for t in deps if not (t[0] == "e" and t[1] == "pe")]
        self.ops[eng].append(_Op(eng, fn, deps))
        tok = ("e", eng, len(self.ops[eng]) - 1)
        self._record(tok, reads, writes)
        return tok

    def dma(self, queue, out, in_, chan, reads=(), writes=(), **kw):
        if isinstance(chan, tuple):
            base, n = chan
            i = self.rot.get(base, 0)
            self.rot[base] = i + 1
            chan = "%s#%d" % (base, i % n)
        if chan not in self.csem:
            self.csem[chan] = self.stack.enter_context(self.nc.semaphore("ch_%d" % len(self.csem)))
            self.ccnt[chan] = 0
        deps = self._deps(reads, writes)
        if self.ccnt[chan]:
            deps.append(("c", chan, self.ccnt[chan]))
        self.ccnt[chan] += 16
        self.ops[queue].append(_Op(queue, lambda e: e.dma_start(out=out, in_=in_, **kw), deps, chan=chan))
        tok = ("c", chan, self.ccnt[chan])
        self._record(tok, reads, writes)
        return tok

    def emit(self, block, final_tokens=()):
        for e in ENGS:
            for o in self.ops[e]:
                for t in o.deps:
                    if t[0] == "e":
                        self.ops[t[1]][t[2]].marked = True
        for t in final_tokens:
            if t[0] == "e":
                self.ops[t[1]][t[2]].marked = True
        for e in ENGS:
            c = 0
            for o in self.ops[e]:
                if o.marked:
                    c += 1
                    o.val = c

        def resolve(t):
            if t[0] == "e":
                return self.psem[t[1]], self.ops[t[1]][t[2]].val, ("e", t[1])
            return self.csem[t[1]], t[2], ("c", t[1])

        def run(e, engobj, extra_final):
            waited = {}
            for o in self.ops[e]:
                need = {}
                for t in o.deps:
                    sem, v, k = resolve(t)
                    if waited.get(k, 0) >= v:
                        continue
                    if k not in need or need[k][1] < v:
                        need[k] = (sem, v)
                for k, (sem, v) in need.items():
                    engobj.wait_ge(sem, v)
                    waited[k] = v
                if o.fn is None:
                    continue
                ins = o.fn(engobj)
                if o.chan is not None:
                    ins.then_inc(self.csem[o.chan], 16)
                elif o.marked:
                    ins.then_inc(self.psem[e], 1)
            if extra_final:
                for t in final_tokens:
                    sem, v, k = resolve(t)
                    engobj.wait_ge(sem, v)

        @block.tensor
        def _(eng):
            run("pe", eng, False)

        @block.scalar
        def _(eng):
            run("act", eng, False)

        @block.vector
        def _(eng):
            run("dve", eng, False)

        @block.gpsimd
        def _(eng):
            run("pool", eng, False)

        @block.sync
        def _(eng):
            run("sp", eng, True)


F32 = mybir.dt.float32
BF16 = mybir.dt.bfloat16
AF = mybir.ActivationFunctionType
ALU = mybir.AluOpType
D = 2048
DC = 16
FFN = 5632
FC = 44
EPS = 1e-6


class Ctx:
    def __init__(self, nc, st, S):
        self.nc, self.st, self.S = nc, st, S
        self.n = 0

    def sb(self, shape, dt, name=None):
        self.n += 1
        return self.st.enter_context(self.nc.sbuf_tensor("%s_%d" % (name or "t", self.n), shape, dt))

    def ps(self, shape, dt=F32, name=None):
        self.n += 1
        return self.st.enter_context(self.nc.psum_tensor("%s_%d" % (name or "p", self.n), shape, dt))


def emit_mod(cx, c_col, wada_t, bada_col, mod_out, nlayers):
    S, nc = cx.S, cx.nc
    NJ = nlayers * 96
    cc = cx.sb([128, DC], F32, "cc")
    cact = cx.sb([128, DC, 2], F32, "cact")
    bada = cx.sb([128, NJ], F32, "bada")
    slabs = [cx.sb([128, 2048], F32, "wa%d" % i) for i in range(3)]
    pm = cx.ps([128, 512], F32, "pmod")
    S.dma("sp", cc[:], c_col, "misc", writes=["cc"])
    S.dma("sp", bada[:], bada_col, "misc", writes=["bada"])
    for i in range(2):
        S.op("act", lambda e, i=i: e.activation(out=cact[:, :, i], in_=cc[:], func=AF.Silu), reads=["cc"], writes=["cact"])
    for j in range(NJ):
        s = j % 3
        S.dma("sp", slabs[s][:], wada_t[j], "wa%d" % s, writes=["wa%d" % s])
        col = (j % 96) * 2
        for kc in range(DC):
            S.op("pe", lambda e, s=s, kc=kc, col=col: e.matmul(pm[:, col:col + 2], lhsT=slabs[s][:, kc * 128:(kc + 1) * 128],
                                                                 rhs=cact[:, kc, :], start=(kc == 0), stop=(kc == DC - 1)),
                 reads=["wa%d" % s, "cact"], writes=["pmod"])
        if j % 96 == 95:
            l = j // 96
            pv = pm[:, 0:192].rearrange("p (j t) -> p j t", t=2)[:, :, 0]
            S.op("dve", lambda e, l=l, pv=pv: e.tensor_tensor(out=mod_out[:, l * 96:(l + 1) * 96], in0=pv, in1=bada[:, l * 96:(l + 1) * 96], op=ALU.add),
                 reads=["pmod", "bada"], writes=["mod"])


def emit_ab(cx, ln_col, mod, sc_off, sh_off, A, B, key):
    S = cx.S
    if sc_off is None:
        S.op("dve", lambda e: e.tensor_copy(out=A[:], in_=ln_col[:]), reads=["lncols"], writes=[key])
        S.op("dve", lambda e: e.memset(B[:], 0.0), writes=[key], reads=[])
    else:
        S.op("dve", lambda e: e.scalar_tensor_tensor(out=A[:], in0=mod[:, sc_off:sc_off + 16], scalar=1.0, in1=ln_col[:], op0=ALU.add, op1=ALU.mult),
             reads=["mod", "lncols"], writes=[key])
        S.op("dve", lambda e: e.tensor_copy(out=B[:], in_=mod[:, sh_off:sh_off + 16]), reads=["mod"], writes=[key])


class Dense:
    def __init__(self, cx, TB):
        self.cx, self.TB = cx, TB
        self.NN = TB // 512
        S = cx.S
        self.ones = cx.sb([128, 128], F32, "ones")
        S.op("pool", lambda e: e.memset(self.ones[:], 1.0), writes=["ones"])
        self.hbuf = cx.sb([128, DC, TB], BF16, "hbuf")
        self.rr = cx.sb([128, TB], F32, "rr")
        self.sq = [cx.sb([128, 512], F32, "sq%d" % i) for i in range(2)]
        self.xc = [cx.sb([128, TB], F32, "xc%d" % i) for i in range(3)]
        self.wb = [cx.sb([128, 2048], BF16, "wb%d" % i) for i in range(4)]
        self.pss = [cx.ps([128, 512], F32, "pss%d" % i) for i in range(self.NN)]
        self.pmm = [cx.ps([128, 512], F32, "pmm%d" % i) for i in range(4)]
        self.ixc = 0
        self.iwb = 0
        self.ipm = 0
        self.isq = 0

    def next_xc(self):
        i = self.ixc % 3
        self.ixc += 1
        return i

    def load_w(self, src, ncols=2048):
        s = self.iwb % 4
        self.iwb += 1
        self.cx.S.dma("pool", self.wb[s][:, 0:ncols], src, "wb%d" % s, writes=["wb%d" % s])
        return s

    def next_pm(self):
        i = self.ipm % 4
        self.ipm += 1
        return i

    def acc_sumsq(self, xi, first, last):
        S = self.cx.S
        for n in range(self.NN):
            q = self.isq % 2
            self.isq += 1
            S.op("act", lambda e, q=q, n=n: e.activation(out=self.sq[q][:], in_=self.xc[xi][:, n * 512:(n + 1) * 512], func=AF.Square),
                 reads=["xc%d" % xi], writes=["sq%d" % q])
            S.op("pe", lambda e, q=q, n=n: e.matmul(self.pss[n][:], lhsT=self.ones[:], rhs=self.sq[q][:], start=first, stop=last),
                 reads=["ones", "sq%d" % q], writes=["pss%d" % n])

    def finish_rstd(self):
        S = self.cx.S
        for n in range(self.NN):
            S.op("act", lambda e, n=n: e.activation(out=self.rr[:, n * 512:(n + 1) * 512], in_=self.pss[n][:], func=AF.Sqrt, scale=1.0 / D, bias=EPS),
                 reads=["pss%d" % n], writes=["rr"])
        S.op("dve", lambda e: e.reciprocal(out=self.rr[:], in_=self.rr[:]), reads=["rr"], writes=["rr"])

    def sumsq_pass(self, x_dram, t0):
        S = self.cx.S
        for dc in range(DC):
            xi = self.next_xc()
            S.dma("sp", self.xc[xi][:], x_dram.ap[dc * 128:(dc + 1) * 128, t0:t0 + self.TB], "xc%d" % xi,
                  reads=[("xd", x_dram.name, dc, t0)], writes=["xc%d" % xi])
            self.acc_sumsq(xi, dc == 0, dc == DC - 1)

    def norm_pass(self, x_dram, t0, A, B, key, out_dram=None, out_dt=BF16):
        S = self.cx.S
        self.finish_rstd()
        for dc in range(DC):
            xi = self.next_xc()
            S.dma("sp", self.xc[xi][:], x_dram.ap[dc * 128:(dc + 1) * 128, t0:t0 + self.TB], "xc%d" % xi,
                  reads=[("xd", x_dram.name, dc, t0)], writes=["xc%d" % xi])
            S.op("dve", lambda e, xi=xi: e.tensor_tensor(out=self.xc[xi][:], in0=self.xc[xi][:], in1=self.rr[:], op=ALU.mult),
                 reads=["xc%d" % xi, "rr"], writes=["xc%d" % xi])
            if out_dram is not None and out_dt == F32:
                S.op("act", lambda e, xi=xi, dc=dc: e.activation(out=self.xc[xi][:], in_=self.xc[xi][:], func=AF.Identity,
                                                                   scale=A[:, dc:dc + 1], bias=B[:, dc:dc + 1]),
                     reads=["xc%d" % xi, key], writes=["xc%d" % xi])
                S.dma("sp", out_dram.ap[dc * 128:(dc + 1) * 128, t0:t0 + self.TB], self.xc[xi][:], ("ow", 3), reads=["xc%d" % xi],
                      writes=[("xd", out_dram.name, dc, t0)])
            else:
                S.op("act", lambda e, xi=xi, dc=dc: e.activation(out=self.hbuf[:, dc, :], in_=self.xc[xi][:], func=AF.Identity,
                                                                   scale=A[:, dc:dc + 1], bias=B[:, dc:dc + 1]),
                     reads=["xc%d" % xi, key], writes=[("h", dc)])
                if out_dram is not None:
                    S.dma("sp", out_dram.ap[dc * 128:(dc + 1) * 128, t0:t0 + self.TB], self.hbuf[:, dc, :], ("ow", 3), reads=[("h", dc)],
                          writes=[("xd", out_dram.name, dc, t0)])

    def proj_residual(self, w_t, nkc, rhs_fn, rhs_keys, x_in, x_out, t0, gate, goff):
        S = self.cx.S
        TB = self.TB
        for mc in range(DC):
            pieces = []
            k0 = 0
            while k0 < nkc:
                kn = min(16, nkc - k0)
                pieces.append((k0, kn, self.load_w(w_t[mc][:, k0 * 128:(k0 + kn) * 128], kn * 128)))
                k0 += kn
            xi = self.next_xc()
            S.dma("sp", self.xc[xi][:], x_in.ap[mc * 128:(mc + 1) * 128, t0:t0 + TB], "xc%d" % xi,
                  reads=[("xd", x_in.name, mc, t0)], writes=["xc%d" % xi])
            for n in range(self.NN):
                pb = self.next_pm()
                for (k0, kn, s) in pieces:
                    for kk in range(kn):
                        kc = k0 + kk
                        S.op("pe", lambda e, s=s, kk=kk, kc=kc, n=n, pb=pb: e.matmul(self.pmm[pb][:], lhsT=self.wb[s][:, kk * 128:(kk + 1) * 128],
                                                                                   rhs=rhs_fn(kc, n), start=(kc == 0), stop=(kc == nkc - 1)),
                             reads=["wb%d" % s, rhs_keys(kc, n)], writes=["pmm%d" % pb])
                S.op("dve", lambda e, n=n, pb=pb, xi=xi, mc=mc: e.scalar_tensor_tensor(
                    out=self.xc[xi][:, n * 512:(n + 1) * 512], in0=self.pmm[pb][:], scalar=gate[:, goff + mc:goff + mc + 1],
                    in1=self.xc[xi][:, n * 512:(n + 1) * 512], op0=ALU.mult, op1=ALU.add),
                    reads=["pmm%d" % pb, "xc%d" % xi, "mod"], writes=["xc%d" % xi])
            self.acc_sumsq(xi, mc == 0, mc == DC - 1)
            S.dma("sp", x_out.ap[mc * 128:(mc + 1) * 128, t0:t0 + TB], self.xc[xi][:], ("xw", 3), reads=["xc%d" % xi],
                  writes=[("xd", x_out.name, mc, t0)])

    def gate_up(self, wgu_t, act):
        S = self.cx.S
        if not hasattr(self, "gtmp"):
            self.gtmp = [self.cx.sb([128, 512], F32, "gtmp%d" % i) for i in range(2)]
            self.ig = 0
        for fc in range(FC):
            sg = self.load_w(wgu_t[fc])
            su = self.load_w(wgu_t[FC + fc])
            for n in range(self.NN):
                pg = self.next_pm()
                pu = self.next_pm()
                for (pb, s) in ((pg, sg), (pu, su)):
                    for kc in range(DC):
                        S.op("pe", lambda e, s=s, kc=kc, n=n, pb=pb: e.matmul(self.pmm[pb][:], lhsT=self.wb[s][:, kc * 128:(kc + 1) * 128],
                                                                            rhs=self.hbuf[:, kc, n * 512:(n + 1) * 512], start=(kc == 0), stop=(kc == DC - 1)),
                             reads=["wb%d" % s, ("h", kc)], writes=["pmm%d" % pb])
                g = self.ig % 2
                self.ig += 1
                S.op("act", lambda e, g=g, pg=pg: e.activation(out=self.gtmp[g][:], in_=self.pmm[pg][:], func=AF.Silu),
                     reads=["pmm%d" % pg], writes=["gtmp%d" % g])
                S.op("dve", lambda e, g=g, pu=pu, fc=fc, n=n: e.tensor_tensor(out=act[:, fc, n * 512:(n + 1) * 512], in0=self.gtmp[g][:], in1=self.pmm[pu][:], op=ALU.mult),
                     reads=["gtmp%d" % g, "pmm%d" % pu], writes=[("act", fc, n)])


def relayout_w(W, kdim_first=True):
    K, N = W.shape
    return np.ascontiguousarray(W.reshape(K // 128, 128, N // 128, 128).transpose(2, 1, 0, 3).reshape(N // 128, 128, K))


def col_layout(v):
    return np.ascontiguousarray(v.reshape(-1, 128).T)


class DR:
    def __init__(self, ap, name):
        self.ap, self.name = ap, name


SB = 512
BL = 128


class Rot:
    def __init__(self, tiles, name):
        self.t, self.name, self.i = tiles, name, 0

    def next(self):
        k = self.i % len(self.t)
        self.i += 1
        return self.t[k], "%s%d" % (self.name, k)


def emit_mixer(cx, T, hT, wfm, wtm, convw, hvec, nwcol, sinkv, cosT, sinT, consts, mixT):
    S, nc = cx.S, cx.nc
    NSB = T // SB
    sb, ps = cx.sb, cx.ps
    ident = sb([128, 128], F32, "ident")
    identb = sb([128, 128], BF16, "identb")
    triu = sb([128, 128], F32, "triu")
    ntril = sb([128, 128], F32, "ntril")
    Rm = sb([128, 128], F32, "Rm")
    amask = sb([128, 256], F32, "amask")
    amask0 = sb([128, 256], F32, "amask0")
    ones = sb([128, 128], F32, "ones")
    S.dma("sp", ident[:], consts[0], ("cst", 12), writes=["ident"])
    S.dma("sp", triu[:], consts[1], ("cst", 12), writes=["triu"])
    S.dma("sp", ntril[:], consts[2], ("cst", 12), writes=["ntril"])
    S.dma("sp", Rm[:], consts[3], ("cst", 12), writes=["Rm"])
    S.dma("sp", amask[:, 0:128], consts[4], ("cst", 12), writes=["amask"])
    S.dma("sp", amask[:, 128:256], consts[5], ("cst", 12), writes=["amask"])
    nml = [sb([128, 128], F32, "nml%d" % i) for i in range(3)]
    for i in range(3):
        S.dma("sp", nml[i][:], consts[6 + i], ("cst", 12), writes=[("nml", i)])
    S.op("pool", lambda e: e.memset(ones[:], 1.0), writes=["ones"])
    S.op("dve", lambda e: e.tensor_copy(out=identb[:], in_=ident[:]), reads=["ident"], writes=["identb"])
    S.op("dve", lambda e: e.tensor_copy(out=amask0[:], in_=amask[:]), reads=["amask"], writes=["amask0"])
    S.op("dve", lambda e: e.memset(amask0[:, 0:128], -30000.0), reads=["amask0"], writes=["amask0"])
    wf = sb([128, 11, 2048], BF16, "wf")
    for b in range(11):
        S.dma("pool", wf[:, b, :], wfm[b], ("wfl", 12), writes=[("wf", b)])
    wt = sb([128, 16, 68], BF16, "wt")
    S.dma("pool", wt[:], wtm, ("wfl", 12), writes=["wt"])
    cw = sb([128, 24], F32, "cw")
    S.dma("sp", cw[:], convw, ("cst", 12), writes=["cw"])
    hv = sb([128, 4], F32, "hv")
    S.dma("sp", hv[:], hvec.to_broadcast([128, 4]), ("cst", 12), writes=["hv"])
    nw = sb([128, 1], F32, "nw")
    S.dma("sp", nw[:], nwcol, ("cst", 12), writes=["nw"])
    snk = sb([128, 4], F32, "snk")
    S.dma("sp", snk[:], sinkv.to_broadcast([128, 4]), ("cst", 12), writes=["snk"])
    nealog = sb([128, 2], F32, "nealog")
    S.op("act", lambda e: e.activation(out=nealog[:], in_=hv[:, 0:2], func=AF.Exp), reads=["hv"], writes=["nealog"])
    S.op("dve", lambda e: e.tensor_scalar(out=nealog[:], in0=nealog[:], scalar1=-1.0, scalar2=None, op0=ALU.mult), reads=["nealog"], writes=["nealog"])

    banks = Rot([ps([128, 512], F32, "bk%d" % i) for i in range(7)], "bk")

    class RV:
        def __init__(self, a, b):
            self.a, self.b = a, b

        def next(self):
            t, k = banks.next()
            return t[:, self.a:self.b], k
    pacc = RV(0, 512)
    pbig = RV(0, 512)
    phalf = RV(0, 256)
    pq = RV(0, 128)
    pq2 = RV(0, 128)
    ptb_t = ps([128, 2, 128], BF16, "ptb")
    ptb = Rot([ptb_t], "ptb")

    hsb = Rot([sb([128, DC, SB], BF16, "hsb%d" % i) for i in range(1)], "hsb")
    cin = [sb([128, 3 + SB], F32, "cin%d" % i) for i in range(6)]
    for i in range(6):
        S.op("pool", lambda e, i=i: e.memset(cin[i][:, 0:3], 0.0), writes=[("cin", i)])
    cact = [sb([128, SB], F32, "cact%d" % i) for i in range(6)]
    zs = [sb([128, SB], F32, "zs%d" % i) for i in range(2)]
    qraw = [sb([128, SB], F32, "qraw%d" % i) for i in range(3)]
    qat = [sb([128, SB], BF16, "qat%d" % i) for i in range(2)]
    kbuf = sb([128, BL + SB], BF16, "kbuf")
    S.op("pool", lambda e: e.memset(kbuf[:, 0:BL], 0.0), writes=["kbuf"])
    vbuf = sb([128, 5, 64], BF16, "vbuf")
    S.op("pool", lambda e: e.memset(vbuf[:, 0, :], 0.0), writes=[("vbuf", 0)])
    ab = sb([128, 4, 4], F32, "ab")
    cs_t = sb([128, SB], F32, "cs_t")
    sn_t = sb([128, SB], F32, "sn_t")
    tmpb = Rot([sb([128, SB], F32, "tmpb%d" % i) for i in range(2)], "tmpb")
    outdn = [sb([128, SB], BF16, "outdn%d" % i) for i in range(2)]
    outat = [sb([128, SB], BF16, "outat%d" % i) for i in range(2)]
    Sst = [sb([128, 128], F32, "Sst%d" % i) for i in range(2)]
    for h in range(2):
        S.op("pool", lambda e, h=h: e.memset(Sst[h][:], 0.0), writes=[("S", h)])
    NU = 8
    def ubuf(name, dt=F32, shape=(128, 128)):
        return [sb(list(shape), dt, "%s%d" % (name, u)) for u in range(NU)]
    sc = ubuf("sc", F32, (128, 8))
    E1 = ubuf("E1"); E2 = ubuf("E2"); Nn = ubuf("Nn"); MT = ubuf("MT"); N3 = ubuf("N3"); PT = ubuf("PT")
    Kd = ubuf("Kd"); Kbg = ubuf("Kbg"); Vb = ubuf("Vb"); WT = ubuf("WT"); Uu = ubuf("Uu"); IT = ubuf("IT"); QgT = ubuf("QgT")
    gbb = ubuf("gbb")
    Vn = Rot([sb([128, 128], F32, "Vn%d" % i) for i in range(2)], "Vn")
    oT = Rot([sb([128, 128], F32, "oT%d" % i) for i in range(2)], "oT")
    sm = Rot([sb([128, 256], F32, "sm%d" % i) for i in range(2)], "sm")
    pe_ = Rot([sb([128, 256], F32, "pe%d" % i) for i in range(2)], "pe")
    pn = Rot([sb([128, 256], BF16, "pn%d" % i) for i in range(2)], "pn")
    pT = Rot([sb([128, 2, 128], BF16, "pT%d" % i) for i in range(2)], "pT")
    st = Rot([sb([128, 8], F32, "st%d" % i) for i in range(4)], "st")

    hTv = hT.rearrange("(kc p) t -> p kc t", p=128)

    for sbi in range(NSB):
        t0 = sbi * SB
        hs, hk = hsb.next()
        S.dma("sp", hs[:], hTv[:, :, t0:t0 + SB], hk, writes=[hk])
        S.dma("sp", cs_t[:], cosT[:, t0:t0 + SB], "cs", writes=["cs_t"])
        S.dma("sp", sn_t[:], sinT[:, t0:t0 + SB], "sn", writes=["sn_t"])
        for b in range(11):
            pa, pk = pacc.next()
            for kc in range(DC):
                S.op("pe", lambda e, pa=pa, b=b, kc=kc, hs=hs: e.matmul(pa[:], lhsT=wf[:, b, kc * 128:(kc + 1) * 128], rhs=hs[:, kc, :],
                                                                      start=(kc == 0), stop=(kc == DC - 1)), reads=[("wf", b), hk], writes=[pk])
            if b < 6:
                S.op("act", lambda e, pa=pa, b=b: e.copy(out=cin[b][:, 3:3 + SB], in_=pa[:]), reads=[pk], writes=[("cin", b)])
            elif b < 8:
                S.op("act", lambda e, pa=pa, b=b: e.activation(out=zs[b - 6][:], in_=pa[:], func=AF.Silu), reads=[pk], writes=[("zs", b - 6)])
            else:
                S.op("act", lambda e, pa=pa, b=b: e.copy(out=qraw[b - 8][:], in_=pa[:]), reads=[pk], writes=[("qraw", b - 8)])
        for j in range(4):
            pqt, pqk = pq.next()
            for kc in range(DC):
                S.op("pe", lambda e, pqt=pqt, kc=kc, j=j, hs=hs: e.matmul(pqt[:, 0:68], lhsT=hs[:, kc, j * BL:(j + 1) * BL], rhs=wt[:, kc, :],
                                                                        start=(kc == 0), stop=(kc == DC - 1)), reads=["wt", hk], writes=[pqk])
            S.op("act", lambda e, pqt=pqt, j=j: e.copy(out=vbuf[:, j + 1, :], in_=pqt[:, 0:64]), reads=[pqk], writes=[("vbuf", j + 1)])
            S.op("dve", lambda e, pqt=pqt, j=j: e.tensor_copy(out=ab[:, j, :], in_=pqt[:, 64:68]), reads=[pqk], writes=[("ab", j)])
        for r in range(3):
            pb_, pbk = pbig.next()
            S.op("pe", lambda e, pb_=pb_, r=r: e.matmul(pb_[:], lhsT=Rm[:], rhs=qraw[r][:], start=True, stop=True), reads=["Rm", ("qraw", r)], writes=[pbk])
            t1, t1k = tmpb.next()
            S.op("dve", lambda e, t1=t1, r=r: e.scalar_tensor_tensor(out=t1[:], in0=qraw[r][:], scalar=(0.125 if r < 2 else 1.0), in1=cs_t[:], op0=ALU.mult, op1=ALU.mult),
                 reads=[("qraw", r), "cs_t"], writes=[t1k])
            t2, t2k = tmpb.next()
            S.op("dve", lambda e, t2=t2, pb_=pb_, r=r: e.scalar_tensor_tensor(out=t2[:], in0=pb_[:], scalar=(0.125 if r < 2 else 1.0), in1=sn_t[:], op0=ALU.mult, op1=ALU.mult),
                 reads=[pbk, "sn_t"], writes=[t2k])
            if r < 2:
                S.op("dve", lambda e, t1=t1, t2=t2, r=r: e.tensor_tensor(out=qat[r][:], in0=t1[:], in1=t2[:], op=ALU.add), reads=[t1k, t2k], writes=[("qat", r)])
            else:
                S.op("dve", lambda e, t1=t1, t2=t2: e.tensor_tensor(out=kbuf[0:64, BL:BL + SB], in0=t1[0:64, :], in1=t2[0:64, :], op=ALU.add), reads=[t1k, t2k, "kbuf"], writes=["kbuf"])
                S.op("act", lambda e: e.copy(out=kbuf[64:128, BL:BL + SB], in_=kbuf[0:64, BL:BL + SB]), reads=["kbuf"], writes=["kbuf"])
        for b in range(6):
            t1, t1k = tmpb.next()
            S.op("dve", lambda e, t1=t1, b=b: e.tensor_scalar(out=t1[:], in0=cin[b][:, 0:SB], scalar1=cw[:, b * 4:b * 4 + 1], scalar2=None, op0=ALU.mult),
                 reads=[("cin", b), "cw"], writes=[t1k])
            for jj in range(1, 4):
                S.op("dve", lambda e, t1=t1, b=b, jj=jj: e.scalar_tensor_tensor(out=t1[:], in0=cin[b][:, jj:jj + SB], scalar=cw[:, b * 4 + jj:b * 4 + jj + 1], in1=t1[:],
                                                                              op0=ALU.mult, op1=ALU.add), reads=[("cin", b), "cw", t1k], writes=[t1k])
            S.op("act", lambda e, t1=t1, b=b: e.activation(out=cact[b][:], in_=t1[:], func=AF.Silu), reads=[t1k], writes=[("cact", b)])
            S.op("act", lambda e, b=b: e.copy(out=cin[b][:, 0:3], in_=cin[b][:, SB:SB + 3]), reads=[("cin", b)], writes=[("cin", b)])
        for b in range(4):
            t1, t1k = tmpb.next()
            S.op("act", lambda e, t1=t1, b=b: e.activation(out=t1[:], in_=cact[b][:], func=AF.Square), reads=[("cact", b)], writes=[t1k])
            pb_, pbk = pbig.next()
            S.op("pe", lambda e, pb_=pb_, t1=t1: e.matmul(pb_[:], lhsT=ones[:], rhs=t1[:], start=True, stop=True), reads=["ones", t1k], writes=[pbk])
            t2, t2k = tmpb.next()
            S.op("act", lambda e, t2=t2, pb_=pb_: e.activation(out=t2[:], in_=pb_[:], func=AF.Sqrt, bias=EPS, scale=1.0), reads=[pbk], writes=[t2k])
            S.op("dve", lambda e, t2=t2: e.reciprocal(out=t2[:], in_=t2[:]), reads=[t2k], writes=[t2k])
            S.op("dve", lambda e, t2=t2, b=b: e.scalar_tensor_tensor(out=cact[b][:], in0=cact[b][:], scalar=(128.0 ** -0.5 if b < 2 else 1.0), in1=t2[:], op0=ALU.mult, op1=ALU.mult),
                 reads=[("cact", b), t2k], writes=[("cact", b)])
        units = [(j, h) for j in range(4) for h in range(2)]
        def ukey(n, u):
            return (n, u)
        for (j, h) in units:
            u = j * 2 + h
            s_ = sc[u]
            k = ukey("sc", u)
            S.op("dve", lambda e, s_=s_, j=j, h=h: e.tensor_tensor(out=s_[:, 5:6], in0=ab[:, j, h:h + 1], in1=hv[:, 2 + h:3 + h], op=ALU.add), reads=[("ab", j), "hv"], writes=[k])
            S.op("act", lambda e, s_=s_: e.activation(out=s_[:, 5:6], in_=s_[:, 5:6], func=AF.Exp), reads=[k], writes=[k])
            S.op("act", lambda e, s_=s_: e.activation(out=s_[:, 5:6], in_=s_[:, 5:6], func=AF.Ln, bias=1.0, scale=1.0), reads=[k], writes=[k])
            S.op("dve", lambda e, s_=s_, h=h: e.tensor_tensor(out=s_[:, 1:2], in0=s_[:, 5:6], in1=nealog[:, h:h + 1], op=ALU.mult), reads=[k, "nealog"], writes=[k])
            S.op("act", lambda e, s_=s_, j=j, h=h: e.activation(out=s_[:, 0:1], in_=ab[:, j, 2 + h:3 + h], func=AF.Exp, scale=-1.0), reads=[("ab", j), k], writes=[k])
            S.op("dve", lambda e, s_=s_: e.tensor_scalar(out=s_[:, 0:1], in0=s_[:, 0:1], scalar1=1.0, scalar2=None, op0=ALU.add), reads=[k], writes=[k])
            S.op("dve", lambda e, s_=s_: e.reciprocal(out=s_[:, 0:1], in_=s_[:, 0:1]), reads=[k], writes=[k])
            p1, p1k = pq.next()
            S.op("pe", lambda e, p1=p1, s_=s_: e.matmul(p1[:, 0:2], lhsT=triu[:], rhs=s_[:, 0:2], start=True, stop=True), reads=["triu", k], writes=[p1k])
            S.op("pe", lambda e, p1=p1, s_=s_: e.matmul(p1[:, 2:4], lhsT=ones[:], rhs=s_[:, 0:2], start=True, stop=True), reads=["ones", k], writes=[p1k])
            S.op("dve", lambda e, p1=p1, s_=s_: e.tensor_copy(out=s_[:, 2:3], in_=p1[:, 1:2]), reads=[p1k, k], writes=[k])
            S.op("dve", lambda e, p1=p1, s_=s_: e.tensor_copy(out=s_[:, 7:8], in_=p1[:, 3:4]), reads=[p1k, k], writes=[k])
            S.op("act", lambda e, s_=s_: e.activation(out=s_[:, 6:7], in_=s_[:, 7:8], func=AF.Exp), reads=[k], writes=[k])
            S.op("act", lambda e, s_=s_: e.activation(out=s_[:, 3:4], in_=s_[:, 2:3], func=AF.Exp, scale=-1.0, bias=s_[:, 7:8]), reads=[k], writes=[k])
            S.op("act", lambda e, s_=s_: e.activation(out=s_[:, 4:5], in_=s_[:, 2:3], func=AF.Exp), reads=[k], writes=[k])
            S.op("dve", lambda e, s_=s_: e.tensor_tensor(out=s_[:, 4:5], in0=s_[:, 4:5], in1=s_[:, 0:1], op=ALU.mult), reads=[k], writes=[k])
            S.op("dve", lambda e, u=u, s_=s_: e.tensor_scalar(out=gbb[u][:], in0=ones[:], scalar1=s_[:, 1:2], scalar2=None, op0=ALU.mult), reads=["ones", k], writes=[ukey("gbb", u)])
            p2, p2k = pq.next()
            S.op("pe", lambda e, p2=p2, u=u: e.matmul(p2[:], lhsT=gbb[u][:], rhs=triu[:], start=True, stop=True), reads=[ukey("gbb", u), "triu"], writes=[p2k])
            S.op("dve", lambda e, p2=p2, u=u, s_=s_: e.tensor_scalar(out=E1[u][:], in0=p2[:], scalar1=s_[:, 2:3], scalar2=0.0, op0=ALU.subtract, op1=ALU.max), reads=[p2k, k], writes=[ukey("E1", u)])
            S.op("act", lambda e, u=u: e.activation(out=E1[u][:], in_=E1[u][:], func=AF.Exp, scale=-1.0), reads=[ukey("E1", u)], writes=[ukey("E1", u)])
            S.op("dve", lambda e, p2=p2, u=u, s_=s_: e.tensor_scalar(out=E2[u][:], in0=p2[:], scalar1=s_[:, 2:3], scalar2=0.0, op0=ALU.subtract, op1=ALU.min), reads=[p2k, k], writes=[ukey("E2", u)])
            S.op("act", lambda e, u=u: e.activation(out=E2[u][:], in_=E2[u][:], func=AF.Exp), reads=[ukey("E2", u)], writes=[ukey("E2", u)])
            S.op("pool", lambda e, u=u: e.tensor_tensor(out=E2[u][:], in0=E2[u][:], in1=triu[:], op=ALU.mult), reads=[ukey("E2", u), "triu"], writes=[ukey("E2", u)])
            S.op("act", lambda e, p2=p2, u=u: e.activation(out=QgT[u][:], in_=p2[:], func=AF.Exp), reads=[p2k], writes=[ukey("QgT", u)])
            S.op("dve", lambda e, u=u, j=j, h=h: e.tensor_tensor(out=QgT[u][:], in0=QgT[u][:], in1=cact[h][:, j * BL:(j + 1) * BL], op=ALU.mult), reads=[ukey("QgT", u), ("cact", h)], writes=[ukey("QgT", u)])
        for (j, h) in units:
            u = j * 2 + h
            s_ = sc[u]
            k = ukey("sc", u)
            kT = cact[2 + h][:, j * BL:(j + 1) * BL]
            qT = cact[h][:, j * BL:(j + 1) * BL]
            vT = cact[4 + h][:, j * BL:(j + 1) * BL]
            p1, p1k = pq.next()
            S.op("pe", lambda e, p1=p1, kT=kT: e.transpose(p1[:], kT, ident[:]), reads=[("cact", 2 + h), "ident"], writes=[p1k])
            S.op("dve", lambda e, p1=p1, u=u, s_=s_: e.tensor_scalar(out=Kd[u][:], in0=p1[:], scalar1=s_[:, 3:4], scalar2=None, op0=ALU.mult), reads=[p1k, k], writes=[ukey("Kd", u)])
            S.op("dve", lambda e, p1=p1, u=u, s_=s_: e.tensor_scalar(out=Kbg[u][:], in0=p1[:], scalar1=s_[:, 4:5], scalar2=None, op0=ALU.mult), reads=[p1k, k], writes=[ukey("Kbg", u)])
            p2, p2k = pq.next()
            S.op("pe", lambda e, p2=p2, vT=vT: e.transpose(p2[:], vT, ident[:]), reads=[("cact", 4 + h), "ident"], writes=[p2k])
            S.op("dve", lambda e, p2=p2, u=u, s_=s_: e.tensor_scalar(out=Vb[u][:], in0=p2[:], scalar1=s_[:, 0:1], scalar2=None, op0=ALU.mult), reads=[p2k, k], writes=[ukey("Vb", u)])
            p3, p3k = pq.next()
            S.op("pe", lambda e, p3=p3, kT=kT: e.matmul(p3[:], lhsT=kT, rhs=kT, start=True, stop=True), reads=[("cact", 2 + h)], writes=[p3k])
            S.op("dve", lambda e, p3=p3, u=u: e.tensor_tensor(out=WT[u][:], in0=p3[:], in1=E1[u][:], op=ALU.mult), reads=[p3k, ukey("E1", u)], writes=[ukey("WT", u)])
            p4, p4k = pq.next()
            S.op("pe", lambda e, p4=p4, kT=kT, qT=qT: e.matmul(p4[:], lhsT=kT, rhs=qT, start=True, stop=True), reads=[("cact", 2 + h), ("cact", h)], writes=[p4k])
            S.op("dve", lambda e, p4=p4, u=u: e.tensor_tensor(out=IT[u][:], in0=p4[:], in1=E2[u][:], op=ALU.mult), reads=[p4k, ukey("E2", u)], writes=[ukey("IT", u)])
            for (dst, dk_, msk, mk_) in ((Nn[u], ukey("Nn", u), ntril, "ntril"), (gbb[u], ukey("gbb", u), nml[0], ("nml", 0)),
                                         (E2[u], ukey("E2", u), nml[1], ("nml", 1)), (N3[u], ukey("N3", u), nml[2], ("nml", 2))):
                S.op("dve", lambda e, dst=dst, msk=msk, u=u, s_=s_: e.scalar_tensor_tensor(out=dst[:], in0=WT[u][:], scalar=s_[:, 0:1], in1=msk[:], op0=ALU.mult, op1=ALU.mult),
                     reads=[ukey("WT", u), k, mk_], writes=[dk_])
            p5, p5k = pq.next()
            S.op("pe", lambda e, p5=p5, u=u: e.transpose(p5[:], Nn[u][:], ident[:]), reads=[ukey("Nn", u), "ident"], writes=[p5k])
            S.op("act", lambda e, p5=p5, u=u: e.copy(out=MT[u][:], in_=p5[:]), reads=[p5k], writes=[ukey("MT", u)])
            S.op("dve", lambda e, p5=p5, u=u: e.tensor_tensor(out=PT[u][:], in0=p5[:], in1=ident[:], op=ALU.add), reads=[p5k, "ident"], writes=[ukey("PT", u)])
        for it in range(1, 4):
            for (j, h) in units:
                u = j * 2 + h
                pa_, pak = pq.next()
                S.op("pe", lambda e, pa_=pa_, u=u: e.matmul(pa_[:], lhsT=MT[u][:], rhs=Nn[u][:], start=True, stop=True), reads=[ukey("MT", u), ukey("Nn", u)], writes=[pak])
                if it < 3:
                    pb2, pb2k = pq.next()
                    S.op("pe", lambda e, pb2=pb2, u=u: e.matmul(pb2[:], lhsT=Nn[u][:], rhs=MT[u][:], start=True, stop=True), reads=[ukey("MT", u), ukey("Nn", u)], writes=[pb2k])
                S.op("act", lambda e, pa_=pa_, u=u: e.copy(out=Nn[u][:], in_=pa_[:]), reads=[pak], writes=[ukey("Nn", u)])
                if it < 3:
                    S.op("dve", lambda e, pb2=pb2, u=u: e.tensor_copy(out=MT[u][:], in_=pb2[:]), reads=[pb2k], writes=[ukey("MT", u)])
                pc_, pck = pq.next()
                S.op("pe", lambda e, pc_=pc_, u=u: e.matmul(pc_[:], lhsT=Nn[u][:], rhs=PT[u][:], start=True, stop=True), reads=[ukey("Nn", u), ukey("PT", u)], writes=[pck])
                S.op("dve", lambda e, pc_=pc_, u=u: e.tensor_tensor(out=PT[u][:], in0=PT[u][:], in1=pc_[:], op=ALU.add), reads=[pck, ukey("PT", u)], writes=[ukey("PT", u)])
        for lv, NL, nk in ((0, gbb, "gbb"), (1, E2, "E2"), (2, N3, "N3")):
            for (j, h) in units:
                u = j * 2 + h
                pd_, pdk = pq.next()
                S.op("pe", lambda e, pd_=pd_, u=u: e.transpose(pd_[:], PT[u][:], ident[:]), reads=[ukey("PT", u), "ident"], writes=[pdk])
                S.op("act", lambda e, pd_=pd_, u=u: e.copy(out=E1[u][:], in_=pd_[:]), reads=[pdk], writes=[ukey("E1", u)])
                pa_, pak = pq.next()
                S.op("pe", lambda e, pa_=pa_, u=u, NL=NL: e.matmul(pa_[:], lhsT=NL[u][:], rhs=PT[u][:], start=True, stop=True), reads=[ukey(nk, u), ukey("PT", u)], writes=[pak])
                S.op("dve", lambda e, pa_=pa_, u=u: e.tensor_copy(out=MT[u][:], in_=pa_[:]), reads=[pak], writes=[ukey("MT", u)])
                pb_, pbk = pq.next()
                S.op("pe", lambda e, pb_=pb_, u=u: e.matmul(pb_[:], lhsT=E1[u][:], rhs=MT[u][:], start=True, stop=True), reads=[ukey("E1", u), ukey("MT", u)], writes=[pbk])
                S.op("dve", lambda e, pb_=pb_, u=u: e.tensor_tensor(out=PT[u][:], in0=PT[u][:], in1=pb_[:], op=ALU.add), reads=[pbk, ukey("PT", u)], writes=[ukey("PT", u)])
        for (j, h) in units:
            u = j * 2 + h
            p1, p1k = pq.next()
            S.op("pe", lambda e, p1=p1, u=u: e.matmul(p1[:], lhsT=Kbg[u][:], rhs=PT[u][:], start=True, stop=True), reads=[ukey("Kbg", u), ukey("PT", u)], writes=[p1k])
            S.op("act", lambda e, p1=p1, u=u: e.copy(out=WT[u][:], in_=p1[:]), reads=[p1k], writes=[ukey("WT", u)])
            p2, p2k = pq.next()
            S.op("pe", lambda e, p2=p2, u=u: e.matmul(p2[:], lhsT=PT[u][:], rhs=Vb[u][:], start=True, stop=True), reads=[ukey("Vb", u), ukey("PT", u)], writes=[p2k])
            S.op("dve", lambda e, p2=p2, u=u: e.tensor_copy(out=Uu[u][:], in_=p2[:]), reads=[p2k], writes=[ukey("Uu", u)])
        for (j, h) in units:
            u = j * 2 + h
            s_ = sc[u]
            k = ukey("sc", u)
            p1, p1k = pq.next()
            S.op("pe", lambda e, p1=p1, u=u, h=h: e.matmul(p1[:], lhsT=WT[u][:], rhs=Sst[h][:], start=True, stop=True), reads=[ukey("WT", u), ("S", h)], writes=[p1k])
            vn, vnk = Vn.next()
            S.op("dve", lambda e, p1=p1, vn=vn, u=u: e.tensor_tensor(out=vn[:], in0=Uu[u][:], in1=p1[:], op=ALU.subtract), reads=[p1k, ukey("Uu", u)], writes=[vnk])
            po, pok = pq2.next()
            S.op("pe", lambda e, po=po, u=u, h=h: e.matmul(po[:], lhsT=Sst[h][:], rhs=QgT[u][:], start=True, stop=False), reads=[("S", h), ukey("QgT", u)], writes=[pok])
            S.op("pe", lambda e, po=po, u=u, vn=vn: e.matmul(po[:], lhsT=vn[:], rhs=IT[u][:], start=False, stop=True), reads=[vnk, ukey("IT", u)], writes=[pok])
            p3, p3k = pq.next()
            S.op("pe", lambda e, p3=p3, u=u, vn=vn: e.matmul(p3[:], lhsT=Kd[u][:], rhs=vn[:], start=True, stop=True), reads=[vnk, ukey("Kd", u)], writes=[p3k])
            S.op("dve", lambda e, p3=p3, h=h, s_=s_: e.scalar_tensor_tensor(out=Sst[h][:], in0=Sst[h][:], scalar=s_[:, 6:7], in1=p3[:], op0=ALU.mult, op1=ALU.add), reads=[p3k, ("S", h), k], writes=[("S", h)])
            o_, ok_ = oT.next()
            S.op("act", lambda e, po=po, o_=o_: e.copy(out=o_[:], in_=po[:]), reads=[pok], writes=[ok_])
            t1, t1k = tmpb.next()
            S.op("act", lambda e, t1=t1, o_=o_: e.activation(out=t1[:, 0:128], in_=o_[:], func=AF.Square), reads=[ok_], writes=[t1k])
            p4, p4k = pq.next()
            S.op("pe", lambda e, p4=p4, t1=t1: e.matmul(p4[:], lhsT=ones[:], rhs=t1[:, 0:128], start=True, stop=True), reads=["ones", t1k], writes=[p4k])
            S.op("act", lambda e, p4=p4, t1=t1: e.activation(out=t1[:, 128:256], in_=p4[:], func=AF.Sqrt, scale=1.0 / 128, bias=EPS), reads=[p4k, t1k], writes=[t1k])
            S.op("dve", lambda e, t1=t1: e.reciprocal(out=t1[:, 128:256], in_=t1[:, 128:256]), reads=[t1k], writes=[t1k])
            S.op("dve", lambda e, t1=t1, o_=o_: e.tensor_tensor(out=o_[:], in0=o_[:], in1=t1[:, 128:256], op=ALU.mult), reads=[t1k, ok_], writes=[ok_])
            S.op("dve", lambda e, o_=o_, h=h, j=j: e.scalar_tensor_tensor(out=outdn[h][:, j * BL:(j + 1) * BL], in0=o_[:], scalar=nw[:, 0:1], in1=zs[h][:, j * BL:(j + 1) * BL], op0=ALU.mult, op1=ALU.mult),
                 reads=[ok_, "nw", ("zs", h)], writes=[("outdn", h)])
        for h in range(2):
            S.dma("sp", mixT[h * 128:(h + 1) * 128, t0:t0 + SB], outdn[h][:], ("mo", 4), reads=[("outdn", h)])
        for j in range(4):
            first = (sbi == 0 and j == 0)
            am = amask0 if first else amask
            amk = "amask0" if first else "amask"
            for hq in range(4):
                blk, ro = hq // 2, (hq % 2) * 64
                psc, psk = phalf.next()
                S.op("pe", lambda e, psc=psc, blk=blk, ro=ro, j=j: e.matmul(psc[:], lhsT=qat[blk][ro:ro + 64, j * BL:(j + 1) * BL], rhs=kbuf[ro:ro + 64, j * BL:j * BL + 256],
                                                                          start=True, stop=True), reads=[("qat", blk), "kbuf"], writes=[psk])
                sm_, smk = sm.next()
                S.op("dve", lambda e, psc=psc, sm_=sm_, am=am: e.tensor_tensor(out=sm_[:], in0=psc[:], in1=am[:], op=ALU.add), reads=[psk, amk], writes=[smk])
                st_, stk = st.next()
                S.op("dve", lambda e, sm_=sm_, st_=st_: e.reduce_max(out=st_[:, 0:1], in_=sm_[:], axis=mybir.AxisListType.X), reads=[smk], writes=[stk])
                S.op("dve", lambda e, st_=st_, hq=hq: e.tensor_tensor(out=st_[:, 0:1], in0=st_[:, 0:1], in1=snk[:, hq:hq + 1], op=ALU.max), reads=[stk, "snk"], writes=[stk])
                S.op("dve", lambda e, st_=st_: e.tensor_scalar(out=st_[:, 1:2], in0=st_[:, 0:1], scalar1=-1.0, scalar2=None, op0=ALU.mult), reads=[stk], writes=[stk])
                pe2, pek = pe_.next()
                S.op("act", lambda e, pe2=pe2, sm_=sm_, st_=st_: e.activation(out=pe2[:], in_=sm_[:], func=AF.Exp, bias=st_[:, 1:2], scale=1.0, accum_out=st_[:, 2:3]), reads=[smk, stk], writes=[pek, stk])
                S.op("act", lambda e, st_=st_, hq=hq: e.activation(out=st_[:, 3:4], in_=snk[:, hq:hq + 1], func=AF.Exp, bias=st_[:, 1:2], scale=1.0), reads=[stk, "snk"], writes=[stk])
                S.op("dve", lambda e, st_=st_: e.tensor_tensor(out=st_[:, 3:4], in0=st_[:, 3:4], in1=st_[:, 2:3], op=ALU.add), reads=[stk], writes=[stk])
                S.op("dve", lambda e, st_=st_: e.reciprocal(out=st_[:, 3:4], in_=st_[:, 3:4]), reads=[stk], writes=[stk])
                pn_, pnk = pn.next()
                S.op("dve", lambda e, pn_=pn_, pe2=pe2, st_=st_: e.tensor_scalar(out=pn_[:], in0=pe2[:], scalar1=st_[:, 3:4], scalar2=None, op0=ALU.mult), reads=[pek, stk], writes=[pnk])
                pt_, ptk = ptb.next()
                for c in range(2):
                    S.op("pe", lambda e, pt_=pt_, pn_=pn_, c=c: e.transpose(pt_[:, c, :], pn_[:, c * 128:(c + 1) * 128], identb[:]), reads=[pnk, "identb"], writes=[ptk])
                pT_, pTk = pT.next()
                S.op("act", lambda e, pt_=pt_, pT_=pT_: e.copy(out=pT_[:], in_=pt_[:]), reads=[ptk], writes=[pTk])
                po, pok = pq2.next()
                for c in range(2):
                    S.op("pe", lambda e, po=po, pT_=pT_, c=c, j=j: e.matmul(po[0:64, :], lhsT=vbuf[:, j + c, :], rhs=pT_[:, c, :], start=(c == 0), stop=(c == 1)),
                         reads=[("vbuf", j + c), pTk], writes=[pok])
                S.op("act", lambda e, po=po, blk=blk, ro=ro, j=j: e.copy(out=outat[blk][ro:ro + 64, j * BL:(j + 1) * BL], in_=po[0:64, :]), reads=[pok], writes=[("outat", blk)])
        for blk in range(2):
            S.dma("sp", mixT[256 + blk * 128:256 + (blk + 1) * 128, t0:t0 + SB], outat[blk][:], ("mo", 4), reads=[("outat", blk)])
        S.op("act", lambda e: e.copy(out=kbuf[:, 0:BL], in_=kbuf[:, SB:SB + BL]), reads=["kbuf"], writes=["kbuf"])
        S.op("dve", lambda e: e.tensor_copy(out=vbuf[:, 0, :], in_=vbuf[:, 4, :]), reads=[("vbuf", 4)], writes=[("vbuf", 0)])
    return S.chan_tokens("mo")


def mixer_consts():
    i = np.arange(128)
    ident = np.eye(128, dtype=np.float32)
    triu = (i[None, :] >= i[:, None]).astype(np.float32)
    c_, s_ = i[:, None], i[None, :]
    low = s_ < c_
    nbd = -(low & (c_ // 16 == s_ // 16)).astype(np.float32)
    lv = []
    for b in (32, 64, 128):
        lv.append(-(low & (c_ // b == s_ // b) & (c_ // (b // 2) != s_ // (b // 2))).astype(np.float32))
    Rm = np.zeros((128, 128), np.float32)
    for m in range(128):
        if (m % 64) < 32:
            Rm[m + 32, m] = -1.0
        else:
            Rm[m - 32, m] = 1.0
    r = i[:, None]
    jj = np.arange(256)[None, :]
    vis = (jj > r) & (jj <= r + 128)
    am = np.where(vis, 0.0, -30000.0).astype(np.float32)
    return np.stack([ident, triu, nbd, Rm, am[:, :128], am[:, 128:]] + lv)


def rope_tables(T):
    half = 32
    inv = 10000.0 ** (-np.arange(half, dtype=np.float32) * 2.0 / 64)
    ang = np.arange(T, dtype=np.float32)[None, :] * inv[:, None]
    c = np.cos(ang).astype(np.float32)
    s = np.sin(ang).astype(np.float32)
    return np.ascontiguousarray(np.concatenate([c] * 4, axis=0)), np.ascontiguousarray(np.concatenate([s] * 4, axis=0))


def build_mod(NJ):
    nc = bass.Bass("TRN2", target_bir_lowering=False)
    c2 = nc.dram_tensor("c2", [128, 16, 2], F32, kind="ExternalInput").ap()
    wada_s = nc.dram_tensor("wada_s", [NJ, 128, 2048], F32, kind="ExternalInput").ap()
    bada_s = nc.dram_tensor("bada_s", [128, NJ], F32, kind="ExternalInput").ap()
    mod_o = nc.dram_tensor("mod_o", [128, NJ, 2], F32, kind="ExternalOutput").ap()
    with ExitStack() as st:
        S = Sched(nc, st)
        cx = Ctx(nc, st, S)
        cc = cx.sb([128, DC, 2], F32, "cc")
        cact = cx.sb([128, DC, 2], F32, "cact")
        bada = cx.sb([128, NJ], F32, "bada")
        modt = cx.sb([128, NJ, 2], F32, "modt")
        slabs = [cx.sb([128, 2048], F32, "wa%d" % i) for i in range(3)]
        pm = cx.ps([128, 512], F32, "pmod")
        block = st.enter_context(nc.Block())
        S.dma("sp", cc[:], c2, "misc", writes=["cc"])
        S.dma("sp", bada[:], bada_s, "misc", writes=["bada"])
        S.op("act", lambda e: e.activation(out=cact[:], in_=cc[:], func=AF.Silu), reads=["cc"], writes=["cact"])
        for j in range(NJ):
            s = j % 3
            S.dma("sp", slabs[s][:], wada_s[j], "wa%d" % s, writes=["wa%d" % s])
            for kc in range(DC):
                S.op("pe", lambda e, s=s, kc=kc, j=j: e.matmul(pm[:, 2 * j:2 * j + 2], lhsT=slabs[s][:, kc * 128:(kc + 1) * 128],
                                                                 rhs=cact[:, kc, :], start=(kc == 0), stop=(kc == DC - 1)),
                     reads=["wa%d" % s, "cact"], writes=["pmod"])
        pv = pm[:, 0:2 * NJ].rearrange("p (j t) -> p j t", t=2)
        for bb in range(2):
            S.op("dve", lambda e, bb=bb: e.tensor_tensor(out=modt[:, :, bb], in0=pv[:, :, bb], in1=bada[:], op=ALU.add),
                 reads=["pmod", "bada"], writes=["modt"])
        S.dma("sp", mod_o, modt[:], ("ow", 3), reads=["modt"])
        S.emit(block, final_tokens=S.chan_tokens("ow"))
    return nc


def build_A(NT, TB):
    nc = bass.Bass("TRN2", target_bir_lowering=False)
    mod_i = nc.dram_tensor("mod_i", [128, 192], F32, kind="ExternalInput").ap()
    lnm = nc.dram_tensor("lnm", [128, 16], F32, kind="ExternalInput").ap()
    xT = DR(nc.dram_tensor("xT", [D, NT], F32, kind="ExternalInput").ap(), "xT")
    hT = DR(nc.dram_tensor("hT", [D, NT], BF16, kind="ExternalOutput").ap(), "hT")
    with ExitStack() as st:
        S = Sched(nc, st)
        cx = Ctx(nc, st, S)
        mod = cx.sb([128, 192], F32, "mod")
        lnc = cx.sb([128, 16], F32, "lnc")
        A = cx.sb([128, 16], F32, "A")
        B = cx.sb([128, 16], F32, "B")
        block = st.enter_context(nc.Block())
        S.dma("sp", lnc[:], lnm, "misc", writes=["lncols"])
        S.dma("sp", mod[:], mod_i, "misc", writes=["mod"])
        emit_ab(cx, lnc, mod, 16, 0, A, B, "AB")
        dn = Dense(cx, TB)
        for tb in range(NT // TB):
            dn.sumsq_pass(xT, tb * TB)
            dn.norm_pass(xT, tb * TB, A, B, "AB", out_dram=hT, out_dt=BF16)
        S.emit(block, final_tokens=S.chan_tokens("ow"))
    return nc


def build_C(NT, TB, l, last):
    nc = bass.Bass("TRN2", target_bir_lowering=False)
    xT = DR(nc.dram_tensor("xT", [D, NT], F32, kind="ExternalInput").ap(), "xT")
    mixT = nc.dram_tensor("mixT", [D, NT], BF16, kind="ExternalInput").ap()
    mod_i = nc.dram_tensor("mod_i", [128, 192], F32, kind="ExternalInput").ap()
    lnf = nc.dram_tensor("lnf", [128, 16], F32, kind="ExternalInput").ap()
    lnn = nc.dram_tensor("lnn", [128, 16], F32, kind="ExternalInput").ap()
    wout_t = nc.dram_tensor("wout_t", [16, 128, 2048], F32, kind="ExternalInput").ap()
    wgu_t = nc.dram_tensor("wgu_t", [88, 128, 2048], F32, kind="ExternalInput").ap()
    wdn_t = nc.dram_tensor("wdn_t", [16, 128, FFN], F32, kind="ExternalInput").ap()
    xmid = DR(nc.dram_tensor("xmid", [D, NT], F32).ap(), "xmid")
    xfin = DR(nc.dram_tensor("xfin", [D, NT], F32, kind="ExternalOutput").ap(), "xfin")
    if last:
        outp = DR(nc.dram_tensor("outT", [D, NT], F32, kind="ExternalOutput").ap(), "outp")
    else:
        outp = DR(nc.dram_tensor("hT", [D, NT], BF16, kind="ExternalOutput").ap(), "outp")
    with ExitStack() as st:
        S = Sched(nc, st)
        cx = Ctx(nc, st, S)
        mod = cx.sb([128, 192], F32, "mod")
        lnc = cx.sb([128, 32], F32, "lnc")
        A1 = cx.sb([128, 16], F32, "A1")
        B1 = cx.sb([128, 16], F32, "B1")
        A2 = cx.sb([128, 16], F32, "A2")
        B2 = cx.sb([128, 16], F32, "B2")
        block = st.enter_context(nc.Block())
        S.dma("sp", mod[:], mod_i, "misc", writes=["mod"])
        S.dma("sp", lnc[:, 0:16], lnf, "misc", writes=["lncols"])
        S.dma("sp", lnc[:, 16:32], lnn, "misc", writes=["lncols"])
        o = l * 96
        emit_ab(cx, lnc[:, 0:16], mod, o + 64, o + 48, A1, B1, "AB1")
        if last:
            emit_ab(cx, lnc[:, 16:32], mod, None, None, A2, B2, "AB2")
        else:
            emit_ab(cx, lnc[:, 16:32], mod, o + 96 + 16, o + 96 + 0, A2, B2, "AB2")
        dn = Dense(cx, TB)
        act = cx.sb([128, FC, TB], BF16, "act")
        mixv = mixT.rearrange("(kc p) t -> p kc t", p=128)
        for tb in range(NT // TB):
            t0 = tb * TB
            S.dma("sp", dn.hbuf[:], mixv[:, :, t0:t0 + TB], "hload", writes=[("h", kc) for kc in range(DC)])
            dn.proj_residual(wout_t, 16, lambda kc, n: dn.hbuf[:, kc, n * 512:(n + 1) * 512], lambda kc, n: ("h", kc), xT, xmid, t0, mod, o + 32)
            dn.norm_pass(xmid, t0, A1, B1, "AB1")
            dn.gate_up(wgu_t, act)
            dn.proj_residual(wdn_t, FC, lambda kc, n: act[:, kc, n * 512:(n + 1) * 512], lambda kc, n: ("act", kc, n), xmid, xfin, t0, mod, o + 80)
            dn.norm_pass(xfin, t0, A2, B2, "AB2", out_dram=outp, out_dt=(F32 if last else BF16))
        S.emit(block, final_tokens=S.chan_tokens("ow") + S.chan_tokens("xw"))
    return nc

def mixer_inputs(w_in, conv_w, a_log, dt_bias, norm_w, sinks, g):
    DNW = 1024
    cols = []
    for part in range(3):
        for h in range(2):
            c0 = part * DNW + (2 * g + h) * 128
            cols.append(np.arange(c0, c0 + 128))
    for h in range(2):
        c0 = 3 * DNW + (2 * g + h) * 128
        cols.append(np.arange(c0, c0 + 128))
    atq0 = 4 * DNW + 16
    for blk in range(2):
        c0 = atq0 + (4 * g + 2 * blk) * 64
        cols.append(np.arange(c0, c0 + 128))
    kvh = g // 2
    k0 = atq0 + 1024 + kvh * 64
    v0 = atq0 + 1024 + 128 + kvh * 64
    cols.append(np.concatenate([np.arange(k0, k0 + 64), np.arange(v0, v0 + 64)]))
    wfm = np.stack([relayout_w(np.ascontiguousarray(w_in[:, c]))[0] for c in cols])
    b0 = 4 * DNW
    tmc = np.concatenate([np.arange(v0, v0 + 64), [b0 + 8 + 2 * g, b0 + 8 + 2 * g + 1], [b0 + 2 * g, b0 + 2 * g + 1]])
    wtm = np.ascontiguousarray(w_in[:, tmc].reshape(16, 128, 68).transpose(1, 0, 2))
    convw = np.zeros((128, 24), np.float32)
    for b in range(6):
        part, h = b // 2, b % 2
        ch = part * DNW + (2 * g + h) * 128 + np.arange(128)
        convw[:, b * 4:(b + 1) * 4] = conv_w[:, ch].T
    hvec = np.array([[a_log[2 * g], a_log[2 * g + 1], dt_bias[2 * g], dt_bias[2 * g + 1]]], np.float32)
    return dict(wfm=wfm, wtm=wtm, convw=convw, hvec=hvec, nwcol=np.ascontiguousarray(norm_w.reshape(128, 1)),
                sinkv=np.ascontiguousarray(sinks[4 * g:4 * g + 4].reshape(1, 4)))


def build_B(T):
    nc = bass.Bass("TRN2", target_bir_lowering=False)
    di = lambda n, s, d=F32: nc.dram_tensor(n, s, d, kind="ExternalInput").ap()
    hT = di("hT", [2048, T], BF16)
    wfm = di("wfm", [11, 128, 2048]); wtm = di("wtm", [128, 16, 68]); convw = di("convw", [128, 24])
    hvec = di("hvec", [1, 4]); nwcol = di("nwcol", [128, 1]); sinkv = di("sinkv", [1, 4])
    cosT = di("cosT", [128, T]); sinT = di("sinT", [128, T]); consts = di("consts", [9, 128, 128])
    mixT = nc.dram_tensor("mixT", [512, T], BF16, kind="ExternalOutput").ap()
    with ExitStack() as st:
        S = Sched(nc, st)
        cx = Ctx(nc, st, S)
        block = st.enter_context(nc.Block())
        tok = emit_mixer(cx, T, hT, wfm, wtm, convw, hvec, nwcol, sinkv, cosT, sinT, consts, mixT)
        S.emit(block, final_tokens=tok)
    return nc


IN_COLS = 5392


def kernel(x, c, ln_mix, ln_ffn, w_ada, b_ada, w_in, dn_conv_w, dn_a_log, dn_dt_bias, dn_norm_w, attn_sinks,
           w_out, w_gate_up, w_down, ln_final):
    f32 = lambda a: np.ascontiguousarray(np.asarray(a, dtype=np.float32))
    x, c, ln_mix, ln_ffn, w_ada, b_ada, w_in = map(f32, (x, c, ln_mix, ln_ffn, w_ada, b_ada, w_in))
    dn_conv_w, dn_a_log, dn_dt_bias, dn_norm_w, attn_sinks = map(f32, (dn_conv_w, dn_a_log, dn_dt_bias, dn_norm_w, attn_sinks))
    w_out, w_gate_up, w_down, ln_final = map(f32, (w_out, w_gate_up, w_down, ln_final))
    NB, T, Dm = x.shape
    NG = 4
    NT = T // NG
    TB = 1024
    cores = list(range(NB * NG))
    NJc = 12
    c2 = np.ascontiguousarray(np.stack([col_layout(c[0]), col_layout(c[1])], axis=2))
    wt_l = [relayout_w(w_ada[l]) for l in range(2)]
    bc_l = [col_layout(b_ada[l]) for l in range(2)]
    inM = []
    for r in cores:
        sl = slice(r * NJc, (r + 1) * NJc)
        inM.append({"c2": c2, "wada_s": np.ascontiguousarray(np.concatenate([wt_l[0][sl], wt_l[1][sl]])),
                    "bada_s": np.ascontiguousarray(np.concatenate([bc_l[0][:, sl], bc_l[1][:, sl]], axis=1))})
    resM = run_bass_kernel_spmd(build_mod(2 * NJc), inM, core_ids=cores).results
    del wt_l, inM
    modb = [np.empty((128, 192), np.float32) for _ in range(NB)]
    for r in cores:
        mo = resM[r]["mod_o"]
        for l in range(2):
            for bb in range(NB):
                modb[bb][:, l * 96 + r * NJc:l * 96 + (r + 1) * NJc] = mo[:, l * NJc:(l + 1) * NJc, bb]
    mod = [modb[r // NG] for r in cores]
    xT = [np.ascontiguousarray(x[r // NG, (r % NG) * NT:(r % NG + 1) * NT, :].T) for r in cores]
    resA = run_bass_kernel_spmd(build_A(NT, TB), [{"mod_i": mod[r], "lnm": col_layout(ln_mix[0]), "xT": xT[r]} for r in cores], core_ids=cores).results
    hT = [resA[r]["hT"] for r in cores]
    cs, sn = rope_tables(T)
    consts = mixer_consts()
    ncB = build_B(T)
    out = None
    for l in range(2):
        last = l == 1
        hfull = [np.ascontiguousarray(np.concatenate([hT[b * NG + g] for g in range(NG)], axis=1)) for b in range(NB)]
        inB = []
        for r in cores:
            d = mixer_inputs(w_in[l], dn_conv_w[l], dn_a_log[l], dn_dt_bias[l], dn_norm_w[l], attn_sinks[l], r % NG)
            d.update(hT=hfull[r // NG], cosT=cs, sinT=sn, consts=consts)
            inB.append(d)
        resB = run_bass_kernel_spmd(ncB, inB, core_ids=cores).results
        del hfull, inB
        mixfull = []
        for b in range(NB):
            dn = np.concatenate([resB[b * NG + g]["mixT"][0:256] for g in range(NG)], axis=0)
            at = np.concatenate([resB[b * NG + g]["mixT"][256:512] for g in range(NG)], axis=0)
            mixfull.append(np.concatenate([dn, at], axis=0))
        wout_t = relayout_w(w_out[l])
        wgu_t = relayout_w(w_gate_up[l])
        wdn_t = relayout_w(w_down[l])
        lnn = col_layout(ln_final if last else ln_mix[l + 1])
        ncC = build_C(NT, TB, l, last)
        inC = [{"xT": xT[r], "mixT": np.ascontiguousarray(mixfull[r // NG][:, (r % NG) * NT:(r % NG + 1) * NT]), "mod_i": mod[r],
                "lnf": col_layout(ln_ffn[l]), "lnn": lnn, "wout_t": wout_t, "wgu_t": wgu_t, "wdn_t": wdn_t} for r in cores]
        resC = run_bass_kernel_spmd(ncC, inC, core_ids=cores).results
        del inC, wout_t, wgu_t, wdn_t, mixfull
        xT = [resC[r]["xfin"] for r in cores]
        if last:
            out = np.empty((NB, T, Dm), np.float32)
            for r in cores:
                out[r // NG, (r % NG) * NT:(r % NG + 1) * NT, :] = resC[r]["outT"].T
        else:
            hT = [resC[r]["hT"] for r in cores]
    return out
```
